# Optimizing a Trainium2 kernel written in Bass

```python
import jax, jax.numpy as jnp
from jax import lax
import numpy as np

D_MODEL = 1024
BATCH = 8
SEQ = 2048
DEPTH = 4
DEC_BATCH = 32
DEC_SEQ = 8
PAST_LEN = 8192
PAGE_SIZE = 128

N_A = DEPTH // 2
N_B = DEPTH - N_A
POOL_WINDOWS = (2, 4, 8, 16)
N_POOL_GROUPS = len(POOL_WINDOWS)
POOL_GROUP_DIM = D_MODEL // N_POOL_GROUPS
POOL_BUF = max(POOL_WINDOWS) - 1
HEAD_DIM = 64
N_HEADS = D_MODEL // HEAD_DIM
DIL_GROUPS = ((128, 1), (512, 4), (2048, 16))
N_DIL = len(DIL_GROUPS)
MAX_WINDOW = max(w for w, _ in DIL_GROUPS)
BAND = max(w // d for w, d in DIL_GROUPS)
D_FF = 2816
RMS_EPS = 1e-6
NEG_INF = -1e30
ATTN_SCALE = HEAD_DIM ** -0.5

kernel_name = 'yoco_pool_dilated_attn_step'


def alibi_slopes(n):
    return 2.0 ** (-8.0 * jnp.arange(1, n + 1, dtype=jnp.float32) / n)


def rmsnorm(x, g):
    xf = x.astype(jnp.float32)
    y = xf * lax.rsqrt(jnp.mean(xf * xf, axis=-1, keepdims=True) + RMS_EPS)
    return (y * g.astype(jnp.float32)).astype(x.dtype)


def swiglu(x, w_gate, w_up, w_down):
    return (jax.nn.silu(x @ w_gate) * (x @ w_up)) @ w_down


def causal_multiscale_pool(u, prev, pos0):
    b, s, _ = u.shape
    n_prev = prev.shape[1]
    ext = jnp.concatenate([prev, u], axis=1).astype(jnp.float32)
    csum = jnp.pad(jnp.cumsum(ext, axis=1), ((0, 0), (1, 0), (0, 0)))
    end = n_prev + jnp.arange(s) + 1
    pos = pos0 + jnp.arange(s)
    groups = []
    for gi, win in enumerate(POOL_WINDOWS):
        ch = slice(gi * POOL_GROUP_DIM, (gi + 1) * POOL_GROUP_DIM)
        tot = csum[:, end, ch] - csum[:, jnp.maximum(end - win, 0), ch]
        cnt = jnp.minimum(pos + 1, win).astype(jnp.float32)
        groups.append(tot / cnt[None, :, None])
    pooled = jnp.stack(groups, axis=2)
    return pooled - u.astype(jnp.float32).reshape(b, s, N_POOL_GROUPS, POOL_GROUP_DIM)


def pool_mixer(hn, prev, pos0, w_in, w_grp, scale, w_out):
    b, s, d = hn.shape
    u = hn @ w_in
    z = causal_multiscale_pool(u, prev, pos0).astype(hn.dtype)
    z = jnp.einsum('bsgc,gcd->bsgd', z, w_grp).reshape(b, s, d)
    y = (z * scale) @ w_out
    rows = jnp.concatenate([prev, u], axis=1)[:, -POOL_BUF:]
    return y, rows


def softmax_stats(s):
    m = jnp.max(s, axis=-1, keepdims=True)
    p = jnp.exp(s - m)
    l = jnp.sum(p, axis=-1, keepdims=True)
    return p / l, (m + jnp.log(l))[..., 0]


def dilated_band_attn(q, k, v, dil, n_steps, slopes):
    b, s, h, dh = q.shape
    blk = dil * BAND
    sp = -(-s // blk) * blk
    n = sp // dil
    nb = n // BAND

    def split(t):
        t = jnp.pad(t, ((0, 0), (0, sp - s), (0, 0), (0, 0)))
        t = t.reshape(b, n, dil, h, dh).transpose(0, 2, 1, 3, 4)
        return t.reshape(b, dil, nb, BAND, h, dh)

    def with_prev(t):
        prev = jnp.pad(t, ((0, 0), (0, 0), (1, 0), (0, 0), (0, 0), (0, 0)))[:, :, :-1]
        return jnp.concatenate([prev, t], axis=3)

    qb = split(q)
    kk = with_prev(split(k))
    vv = with_prev(split(v))
    sc = jnp.einsum('brnqhe,brnkhe->brnhqk', qb, kk).astype(jnp.float32) * ATTN_SCALE
    qi = jnp.arange(BAND)[:, None]
    ki = jnp.arange(2 * BAND)[None, :]
    step = qi + BAND - ki
    kstep = jnp.arange(nb)[:, None, None] * BAND + ki[None] - BAND
    valid = (step >= 0)[None] & (step <= n_steps)[None] & (kstep >= 0)
    bias = -slopes[:, None, None] * (step * dil).astype(jnp.float32)[None]
    sc = jnp.where(valid[:, None], sc + bias, NEG_INF)
    p, lse = softmax_stats(sc)
    o = jnp.einsum('brnhqk,brnkhe->brnqhe', p.astype(v.dtype), vv)
    o = o.reshape(b, dil, n, h, dh).transpose(0, 2, 1, 3, 4).reshape(b, sp, h, dh)[:, :s]
    lse = lse.transpose(0, 1, 2, 4, 3).reshape(b, dil, n, h).transpose(0, 2, 1, 3).reshape(b, sp, h)[:, :s]
    return o, lse


def dilated_gather_attn(q, k_all, v_all, n_past, dil, n_steps, slopes):
    t = q.shape[1]
    dist = jnp.arange(n_steps + 1) * dil
    idx = n_past + jnp.arange(t)[:, None] - dist[None, :]
    valid = idx >= 0
    idx = jnp.maximum(idx, 0)
    kg = k_all[:, idx]
    vg = v_all[:, idx]
    sc = jnp.einsum('bthe,btkhe->bhtk', q, kg).astype(jnp.float32) * ATTN_SCALE
    sc = sc - slopes[:, None, None] * dist.astype(jnp.float32)[None, None, :]
    sc = jnp.where(valid[None, None], sc, NEG_INF)
    p, lse = softmax_stats(sc)
    o = jnp.einsum('bhtk,btkhe->bthe', p.astype(v_all.dtype), vg)
    return o, lse.transpose(0, 2, 1)


def dilated_mixer(hn, w_q, w_o, k_all, v_all, n_past, banded, slopes):
    b, s, d = hn.shape
    q = (hn @ w_q).reshape(b, s, N_DIL, N_HEADS, HEAD_DIM)
    outs, lses = [], []
    for gi, (win, dil) in enumerate(DIL_GROUPS):
        if banded:
            o, lse = dilated_band_attn(q[:, :, gi], k_all, v_all, dil, win // dil, slopes)
        else:
            o, lse = dilated_gather_attn(q[:, :, gi], k_all, v_all, n_past, dil, win // dil, slopes)
        outs.append(o)
        lses.append(lse)
    wts = jax.nn.softmax(jnp.stack(lses), axis=0)
    o = jnp.sum(wts[..., None] * jnp.stack(outs).astype(jnp.float32), axis=0)
    return o.reshape(b, s, d).astype(hn.dtype) @ w_o


def trunk(x, pool_prev, k_past, v_past, pos0, norm_g, ffn_w_gate, ffn_w_up, ffn_w_down,
          pool_w_in, pool_w_grp, pool_scale, pool_w_out, kv_norm, w_k, w_v, attn_w_q, attn_w_o):
    b, s, _ = x.shape
    slopes = alibi_slopes(N_HEADS)
    banded = k_past is None
    h = x
    pool_rows = []
    k_all = v_all = None
    n_past = 0
    buf = 0
    for layer in range(DEPTH):
        if layer == N_A:
            kvn = rmsnorm(h, kv_norm)
            k_new = (kvn @ w_k).reshape(b, s, N_HEADS, HEAD_DIM)
            v_new = (kvn @ w_v).reshape(b, s, N_HEADS, HEAD_DIM)
            if banded:
                k_all, v_all = k_new, v_new
                buf = min(MAX_WINDOW, s)
            else:
                n_past = k_past.shape[1]
                k_all = jnp.concatenate([k_past, k_new], axis=1)
                v_all = jnp.concatenate([v_past, v_new], axis=1)
                buf = n_past
        g = norm_g[layer]
        f = swiglu(rmsnorm(h, g[0]), ffn_w_gate[layer, 0], ffn_w_up[layer, 0], ffn_w_down[layer, 0])
        h = h + 0.5 * rmsnorm(f, g[1])
        hn = rmsnorm(h, g[2])
        if layer < N_A:
            y, rows = pool_mixer(hn, pool_prev[layer], pos0, pool_w_in[layer], pool_w_grp[layer],
                                 pool_scale[layer], pool_w_out[layer])
            pool_rows.append(rows)
        else:
            j = layer - N_A
            y = dilated_mixer(hn, attn_w_q[j], attn_w_o[j], k_all, v_all, n_past, banded, slopes)
        h = h + rmsnorm(y, g[3])
        f = swiglu(rmsnorm(h, g[4]), ffn_w_gate[layer, 1], ffn_w_up[layer, 1], ffn_w_down[layer, 1])
        h = h + 0.5 * rmsnorm(f, g[5])
    return h, jnp.stack(pool_rows), k_all[:, -buf:], v_all[:, -buf:]


def setup_inputs(seed: int = 0) -> dict:
    key = jax.random.key(seed)
    ks = jax.random.split(key, 20)
    nrm = jax.random.normal
    kv_buf = min(MAX_WINDOW, PAST_LEN)
    d, f, c = D_MODEL, D_FF, POOL_GROUP_DIM
    return {
        'x_prompt': nrm(ks[0], (BATCH, SEQ, d), jnp.float32),
        'x_sample': nrm(ks[1], (DEC_BATCH, DEC_SEQ, d), jnp.float32),
        'state_pool': nrm(ks[2], (N_A, DEC_BATCH, POOL_BUF, d), jnp.float32),
        'cache_k': nrm(ks[3], (DEC_BATCH, kv_buf, N_HEADS, HEAD_DIM), jnp.float32),
        'cache_v': nrm(ks[4], (DEC_BATCH, kv_buf, N_HEADS, HEAD_DIM), jnp.float32),
        'norm_g': 1.0 + 0.05 * nrm(ks[5], (DEPTH, 6, d), jnp.float32),
        'ffn_w_gate': nrm(ks[6], (DEPTH, 2, d, f), jnp.float32) * d ** -0.5,
        'ffn_w_up': nrm(ks[7], (DEPTH, 2, d, f), jnp.float32) * d ** -0.5,
        'ffn_w_down': nrm(ks[8], (DEPTH, 2, f, d), jnp.float32) * f ** -0.5,
        'pool_w_in': nrm(ks[9], (N_A, d, d), jnp.float32) * d ** -0.5,
        'pool_w_grp': nrm(ks[10], (N_A, N_POOL_GROUPS, c, c), jnp.float32) * c ** -0.5,
        'pool_scale': 1.0 + 0.1 * nrm(ks[11], (N_A, d), jnp.float32),
        'pool_w_out': nrm(ks[12], (N_A, d, d), jnp.float32) * d ** -0.5,
        'kv_norm': 1.0 + 0.05 * nrm(ks[13], (d,), jnp.float32),
        'w_k': nrm(ks[14], (d, d), jnp.float32) * d ** -0.5,
        'w_v': nrm(ks[15], (d, d), jnp.float32) * d ** -0.5,
        'attn_w_q': nrm(ks[16], (N_B, d, N_DIL * d), jnp.float32) * d ** -0.5,
        'attn_w_o': nrm(ks[17], (N_B, d, d), jnp.float32) * d ** -0.5,
    }


def reference(x_prompt, x_sample, state_pool, cache_k, cache_v, norm_g, ffn_w_gate, ffn_w_up,
              ffn_w_down, pool_w_in, pool_w_grp, pool_scale, pool_w_out, kv_norm, w_k, w_v,
              attn_w_q, attn_w_o):
    empty_pool = jnp.zeros((N_A, x_prompt.shape[0], 0, D_MODEL), x_prompt.dtype)
    y_prompt, pool_p, k_p, v_p = trunk(
        x_prompt, empty_pool, None, None, 0, norm_g, ffn_w_gate, ffn_w_up, ffn_w_down,
        pool_w_in, pool_w_grp, pool_scale, pool_w_out, kv_norm, w_k, w_v, attn_w_q, attn_w_o)
    y_sample, pool_s, k_s, v_s = trunk(
        x_sample, state_pool, cache_k, cache_v, PAST_LEN, norm_g, ffn_w_gate, ffn_w_up, ffn_w_down,
        pool_w_in, pool_w_grp, pool_scale, pool_w_out, kv_norm, w_k, w_v, attn_w_q, attn_w_o)
    return (y_prompt, y_sample, pool_p, k_p, v_p, pool_s, k_s, v_s)
```

```python
import contextlib
import numpy as np
import concourse.bass as bass
import concourse.mybir as mybir
from concourse.bass_utils import run_bass_kernel_spmd

F32 = mybir.dt.float32
BF16 = mybir.dt.bfloat16
AF = mybir.ActivationFunctionType
ALU = mybir.AluOpType

D = 1024
NJ = 22
S = 2048
NSM = 32
HT = S + NSM
NTOK = 544
SCALE = 0.125
EPS = 1e-6
BIGN = 1.0e5
NEGB = -30000.0
WINS = (2, 4, 8, 16)
DILS = (1, 4, 16)
NVEC = 27


class Prog:
    ENG = ("pe", "act", "dve", "pool", "sp")

    def __init__(self, nc):
        self.nc = nc
        self.ops = []
        self.last_w = {}
        self.readers = {}

    def op(self, eng, fn, reads=(), writes=(), lane=None, extra=()):
        deps = set(extra)
        for k in reads:
            w = self.last_w.get(k)
            if w is not None:
                deps.add(w)
        for k in writes:
            w = self.last_w.get(k)
            if w is not None:
                deps.add(w)
            rd = self.readers.get(k)
            if rd:
                deps.update(rd.values())
        idx = len(self.ops)
        self.ops.append([eng, fn, deps, lane, False, 0])
        rkey = (eng, idx) if lane is not None else eng
        for k in reads:
            self.readers.setdefault(k, {})[rkey] = idx
        for k in writes:
            self.last_w[k] = idx
            self.readers[k] = {}
        return idx

    def dma(self, queue, lane, fn, reads=(), writes=()):
        return self.op(queue, fn, reads, writes, lane=lane)

    def barrier(self):
        last = {}
        for i, o in enumerate(self.ops):
            if o[1] is None:
                continue
            if o[3] is not None:
                last[("l", o[3])] = i
            else:
                last[("e", o[0])] = i
        deps = set(last.values())
        for e in self.ENG:
            self.op(e, None, extra=deps)

    def emit(self):
        nc = self.nc
        ops = self.ops
        for o in ops:
            for d in o[2]:
                dd = ops[d]
                if dd[3] is None and dd[0] == "pe" and o[0] == "pe" and o[3] is None:
                    continue
                dd[4] = True
        cnt = {e: 0 for e in self.ENG}
        lane_cnt = {}
        for o in ops:
            if o[3] is not None:
                lane_cnt[o[3]] = lane_cnt.get(o[3], 0) + 16
                o[5] = lane_cnt[o[3]]
                o[4] = True
            elif o[4]:
                cnt[o[0]] += 1
                o[5] = cnt[o[0]]
        lane_names = sorted(lane_cnt.keys(), key=str)
        with contextlib.ExitStack() as st:
            sems = {e: st.enter_context(nc.semaphore("s_" + e)) for e in self.ENG if e != "sp"}
            lsem = {}
            for i, ln in enumerate(lane_names):
                lsem[ln] = st.enter_context(nc.semaphore("l%d" % i))
            block = st.enter_context(nc.Block())
            per_eng = {e: [] for e in self.ENG}
            for i, o in enumerate(ops):
                per_eng[o[0]].append(i)

            def run(engname, engobj):
                waited = {}
                for i in per_eng[engname]:
                    eng, fn, deps, lane, sig, val = ops[i]
                    need = {}
                    for d in deps:
                        dd = ops[d]
                        if dd[3] is not None:
                            key = ("l", dd[3])
                        else:
                            if dd[0] == "pe" and engname == "pe" and lane is None:
                                continue
                            key = ("e", dd[0])
                        if dd[5] > need.get(key, 0):
                            need[key] = dd[5]
                    for key, v in need.items():
                        if waited.get(key, 0) >= v:
                            continue
                        waited[key] = v
                        s = lsem[key[1]] if key[0] == "l" else sems[key[1]]
                        engobj.wait_ge(s, v)
                    if fn is None:
                        continue
                    ins = fn(engobj)
                    if lane is not None:
                        ins.then_inc(lsem[lane], 16)
                    elif sig:
                        ins.then_inc(sems[engname], 1)

            @block.tensor
            def _(e):
                run("pe", e)

            @block.scalar
            def _(e):
                run("act", e)

            @block.vector
            def _(e):
                run("dve", e)

            @block.gpsimd
            def _(e):
                run("pool", e)

            @block.sync
            def _(e):
                run("sp", e)


DBG = {}


def slope(h):
    return 2.0 ** (-(h + 1) / 2.0)


def build_program(stop=None):
    nc = bass.Bass("TRN2", target_bir_lowering=False)

    def din(name, shape):
        return nc.dram_tensor(name, list(shape), F32, kind="ExternalInput").ap()

    def dout(name, shape):
        return nc.dram_tensor(name, list(shape), F32, kind="ExternalOutput").ap()

    xT = din("xT", [D, HT])
    spT = din("spT", [2, D, 60])
    sprow = din("sprow", [2, 4, 7, D])
    ck = din("ck", [4, S, D])
    cv = din("cv", [4, S, D])
    gvec_d = din("gvec", [128, 8 * NVEC])
    wgate = din("wgate", [4, 2, D, 2816])
    wup = din("wup", [4, 2, D, 2816])
    wdown = din("wdown", [4, 2, 2816, D])
    pwin = din("pwin", [2, D, D])
    pwgrp = din("pwgrp", [2, 4, 256, 256])
    pwout = din("pwout", [2, D, D])
    wk_d = din("wk", [D, D])
    wv_d = din("wv", [D, D])
    wq_d = din("wq", [2, D, 3 * D])
    wo_d = din("wo", [2, D, D])
    ident_d = din("ident", [128, 128])
    psw_d = din("psw", [128, 128])
    nstab_d = din("nstab", [128, 384])
    rctab_d = din("rctab", [128, 64])
    sbias_d = din("sbias", [25, 128, 128])

    yT = dout("yT", [D, HT])
    pool_p = dout("pool_p", [2, 15, D])
    k_p = dout("k_p", [S, D])
    v_p = dout("v_p", [S, D])
    pool_s = dout("pool_s", [2, 4, 15, D])
    k_s = dout("k_s", [4, S, D])
    v_s = dout("v_s", [4, S, D])

    st = contextlib.ExitStack()
    with st:
        def sb(name, shape, dt):
            return st.enter_context(nc.sbuf_tensor(name, list(shape), dt))

        def psum(name, shape, dt):
            return st.enter_context(nc.psum_tensor(name, list(shape), dt))

        H = sb("H", [128, 8, HT], F32)
        Fb = sb("Fb", [128, 8, NTOK], F32)
        Ab = Fb[:].rearrange("p c n -> p (c n)")[:, 0:4 * NTOK].bitcast(BF16).rearrange("p (c n) -> p c n", c=8)
        BIG = sb("BIG", [128, 24, NTOK], BF16)
        KTr = sb("KTr", [128, 8 * S], BF16)
        KT = KTr[:].rearrange("p (c n) -> p c n", c=8)
        KF = KTr[:].bitcast(F32)
        U = KF[:, 0:8 * 528].rearrange("p (c n) -> p c n", c=8)
        Us = KF[:, 4224:4224 + 8 * 92].rearrange("p (c b n) -> p c b n", c=8, b=4)
        PSa = KF[:, 4960:4960 + 528]
        PSb = KF[:, 5488:5488 + 528]
        SSa = KF[:, 6016:6016 + 92].rearrange("p (b n) -> p b n", b=4)
        SSb = KF[:, 6108:6108 + 92].rearrange("p (b n) -> p b n", b=4)
        T16 = KF[:, 6200:6216]
        TMP_tok = KF[:, 6216:6216 + 1024]
        NW2K = 5
        W2K = [sb("w2k%d" % i, [128, 8, 128], BF16) for i in range(NW2K)]
        WD = [sb("wd%d" % i, [128, NJ, 128], BF16) for i in range(2)]
        NVS = 4
        VS = [sb("vs%d" % i, [128, 1024], BF16) for i in range(NVS)]
        VsN = sb("VsN", [32, 2048], BF16)
        KTs = sb("KTs", [128, 8, 32], BF16)
        GV = sb("GV", [128, 8, NVEC], F32)
        ONES = sb("ONES", [128, 128], BF16)
        IDN = sb("IDN", [128, 128], BF16)
        PSW = sb("PSW", [128, 128], F32)
        NST = sb("NST", [128, 384], F32)
        RCT = sb("RCT", [128, 4, 16], F32)
        EPSV = sb("EPSV", [128, 1], F32)
        SQ = [sb("sq%d" % i, [128, 512], BF16) for i in range(2)]
        TMPS = [sb("tmps%d" % i, [128, 512], F32) for i in range(2)]
        RSTD = [sb("rstd0", [128, 512], F32), sb("rstd1", [128, NSM], F32)]
        A2 = sb("A2", [128, 8, 512], BF16)
        SG = [sb("sg%d" % i, [128, 512], F32) for i in range(2)]
        TT = [sb("tt%d" % i, [128, 256], F32) for i in range(4)]
        PT = [sb("pt%d" % i, [128, 256], BF16) for i in range(4)]
        KS = [sb("ks%d" % i, [128, 4, 128], F32) for i in range(2)]
        KR = [SG[i][:].bitcast(BF16) for i in range(2)]
        KsT = [KS[i][:].rearrange("p t x -> p (t x)").bitcast(BF16).rearrange("p (c n) -> p c n", c=8) for i in range(2)]
        SBI = [sb("sbi%d" % i, [128, 128], F32) for i in range(1)]
        S1 = sb("S1", [128, 128], F32)
        LNB = TMPS[0]
        REC = TMPS[1]

        NB = 5
        PB = [psum("pb%d" % i, [128, 512], F32) for i in range(NB)]
        PST = [psum("pst%d" % i, [128, 512], F32) for i in range(2)]
        PTR = psum("ptr", [128, D], BF16)
        PBX = PB + [PST[1]]

        def bk(b):
            return ("ps", b) if b < NB else ("pst", 1)

        P = Prog(nc)
        ctr = {"bank": 0, "w2k": 0, "wd": 0, "vs": 0, "sq": 0, "sg": 0, "tt": 0, "ks": 0, "kr": 0, "sbi": 0}

        def nxt(name, n):
            v = ctr[name]
            ctr[name] = (v + 1) % n
            return v

        def bank():
            return nxt("bank", NB)

        def kA(k):
            return [("A", k), ("F", k // 2)]

        def kF(m):
            return [("F", m)] + ([("A", 2 * m), ("A", 2 * m + 1)] if m < 4 else [])

        SEGS = []
        for p in range(4):
            sg = [dict(h0=512 * p, n=512, l0=0, id=p, si=0)]
            if p == 3:
                sg.append(dict(h0=S, n=NSM, l0=512, id="s", si=1))
            SEGS.append(sg)

        P.dma("sp", "ld_h", lambda e: e.dma_start(out=H[:], in_=xT.rearrange("(c p) n -> p c n", p=128)),
              writes=[("h", c, sid) for c in range(8) for sid in (0, 1, 2, 3, "s")])
        P.dma("sp", "ld_gv", lambda e: e.dma_start(out=GV[:].rearrange("p c v -> p (c v)"), in_=gvec_d), writes=["GV"])
        P.dma("sp", "ld_psw", lambda e: e.dma_start(out=PSW[:], in_=psw_d), writes=["PSW"])
        P.dma("sp", "ld_nst", lambda e: e.dma_start(out=NST[:], in_=nstab_d), writes=["NST"])
        P.dma("sp", "ld_rct", lambda e: e.dma_start(out=RCT[:].rearrange("p g n -> p (g n)"), in_=rctab_d), writes=["RCT"])
        P.dma("pool", "ld_idn", lambda e: e.dma_start(out=IDN[:], in_=ident_d), writes=["IDN"])
        out_keys = []
        for b in range(0 if not DBG.get("nocc") else 0, 4 if not DBG.get("nocc") else 0):
            for nm, src, dst in (("k", ck, k_s), ("v", cv, v_s)):
                key = ("o", "cc", nm, b)
                P.dma("sp", ("cc", nm, b), lambda e, src=src, dst=dst, b=b: e.dma_start(
                    out=dst[b, 0:S - 8, :].rearrange("r d -> (r d)").rearrange("(a x) -> a x", a=128),
                    in_=src[b, 8:S, :].rearrange("r d -> (r d)").rearrange("(a x) -> a x", a=128)),
                      writes=[key])
                out_keys.append(key)
        for l in range(2 if not DBG.get("nosprow") else 0):
            key = ("o", "sprow", l)
            P.dma("sp", ("sprow", l), lambda e, l=l: e.dma_start(out=pool_s[l, :, 0:7, :], in_=sprow[l]), writes=[key])
            out_keys.append(key)
        P.op("dve", lambda e: e.memset(ONES[:], 1.0), writes=["ONES"])
        P.op("dve", lambda e: e.memset(EPSV[:], EPS), writes=["EPSV"])
        for i in range(NVS):
            P.op("pool", lambda e, i=i: e.memset(VS[i][:].rearrange("p (c x) -> p c x", x=256)[:, :, 64:192], 1.0), writes=[("vsj", i, 0), ("vsj", i, 1)])
        P.op("pool", lambda e: e.memset(VsN[:].rearrange("p (c x) -> p c x", x=256)[:, :, 64:192], 1.0), writes=["VsN"])
        for l in range(4):
            for idx in (l * 6 + 1, l * 6 + 5):
                P.op("dve", lambda e, idx=idx: e.tensor_scalar(out=GV[:, :, idx:idx + 1], in0=GV[:, :, idx:idx + 1], scalar1=0.5,
                                                               scalar2=None, op0=ALU.mult), reads=["GV"], writes=["GV"])

        def w2k_load(src):
            s = nxt("w2k", NW2K)
            P.dma("pool", ("w2k", s), lambda e, s=s, src=src: e.dma_start(out=W2K[s][:], in_=src.rearrange("(c p) m -> p c m", p=128)),
                  writes=[("w2k", s)])
            return s

        def wd_load(src, kind="d"):
            s = nxt("wd", 2)
            if kind == "d":
                for hf in range(2):
                    P.dma("pool", ("wd", s, hf), lambda e, s=s, src=src, hf=hf: e.dma_start(
                        out=WD[s][:, 11 * hf:11 * hf + 11, :],
                        in_=src[1408 * hf:1408 * hf + 1408, :].rearrange("(j p) m -> p j m", p=128)), writes=[("wd", s, hf)])
            else:
                P.dma("pool", ("wd", s, 0), lambda e, s=s, src=src: e.dma_start(
                    out=WD[s][:].rearrange("p j m -> p (j m)")[:, 0:2048].rearrange("p (g k n) -> p g k n", g=4, k=2),
                    in_=src.rearrange("g (k p) n -> p g k n", p=128)), writes=[("wd", s, 0), ("wd", s, 1)])
            return s

        def mm_acc(b, col, n, pairs, reads, M=128, extra_writes=(), first=True, fin=True):
            def f(e, b=b, col=col, n=n, pairs=pairs, M=M, first=first, fin=fin):
                last = len(pairs) - 1
                for i, (l, r) in enumerate(pairs):
                    ins = e.matmul(PBX[b][0:M, col:col + n], lhsT=l, rhs=r, start=(first and i == 0), stop=(fin and i == last))
                return ins
            return P.op("pe", f, reads=reads, writes=[bk(b)] + list(extra_writes))

        def stats_add(src_ap, n, si, first, last, reads, ser=()):
            i = nxt("sq", 2)
            P.op("act", lambda e, i=i, src_ap=src_ap, n=n: e.activation(out=SQ[i][:, 0:n], in_=src_ap, func=AF.Square),
                 reads=reads, writes=[("sq", i)] + list(ser))
            P.op("pe", lambda e, i=i, n=n, si=si, first=first, last=last: e.matmul(
                PST[si][:, 0:n], lhsT=ONES[:], rhs=SQ[i][:, 0:n], start=first, stop=last),
                reads=[("sq", i), "ONES"], writes=[("pst", si)])

        def stats_sq(src_ap, n, reads, ser=()):
            i = nxt("sq", 2)
            P.op("act", lambda e, i=i, src_ap=src_ap, n=n: e.activation(out=SQ[i][:, 0:n], in_=src_ap, func=AF.Square),
                 reads=reads, writes=[("sq", i)] + list(ser))
            return i

        def stats_mm(i, n, si, first, last):
            P.op("pe", lambda e, i=i, n=n, si=si, first=first, last=last: e.matmul(
                PST[si][:, 0:n], lhsT=ONES[:], rhs=SQ[i][:, 0:n], start=first, stop=last),
                reads=[("sq", i), "ONES"], writes=[("pst", si)])

        def stats_finish(n, si):
            P.op("act", lambda e, n=n, si=si: e.activation(out=TMPS[si][:, 0:n], in_=PST[si][:, 0:n], func=AF.Ln, scale=1.0 / D, bias=EPSV[:, 0:1]),
                 reads=[("pst", si), "EPSV"], writes=[("tmps", si)])
            P.op("act", lambda e, n=n, si=si: e.activation(out=RSTD[si][:, 0:n], in_=TMPS[si][:, 0:n], func=AF.Exp, scale=-0.5),
                 reads=[("tmps", si)], writes=[("rstd", si)])

        def prenorm(gidx, segs, a2=False):
            for seg in segs:
                h0, n, l0, sid, si = seg["h0"], seg["n"], seg["l0"], seg["id"], seg["si"]
                for c in range(8):
                    stats_add(H[:, c, h0:h0 + n], n, si, c == 0, c == 7, [("h", c, sid)])
                stats_finish(n, si)
                for c in range(8):
                    dst = A2[:, c, 0:n] if a2 else Ab[:, c, l0:l0 + n]
                    wk = [("A2", c)] if a2 else kA(c)
                    P.op("dve", lambda e, c=c, h0=h0, n=n, si=si, dst=dst: e.scalar_tensor_tensor(
                        out=dst, in0=H[:, c, h0:h0 + n], scalar=GV[:, c, gidx:gidx + 1], in1=RSTD[si][:, 0:n],
                        op0=ALU.mult, op1=ALU.mult), reads=[("h", c, sid), ("rstd", si), "GV"], writes=wk)

        def postnorm(gidx, segs):
            for seg in segs:
                h0, n, l0, sid, si = seg["h0"], seg["n"], seg["l0"], seg["id"], seg["si"]
                stats_finish(n, si)
                for m in range(8):
                    P.op("dve", lambda e, m=m, n=n, l0=l0, si=si: e.scalar_tensor_tensor(
                        out=Fb[:, m, l0:l0 + n], in0=Fb[:, m, l0:l0 + n], scalar=GV[:, m, gidx:gidx + 1], in1=RSTD[si][:, 0:n],
                        op0=ALU.mult, op1=ALU.mult), reads=kF(m) + [("rstd", si), "GV"], writes=kF(m))
                    P.op("dve", lambda e, m=m, h0=h0, n=n, l0=l0: e.tensor_tensor(
                        out=H[:, m, h0:h0 + n], in0=H[:, m, h0:h0 + n], in1=Fb[:, m, l0:l0 + n], op=ALU.add),
                        reads=kF(m) + [("h", m, sid)], writes=[("h", m, sid)])

        def out_proj(wsrc_fn, rhs_chunks, segs):
            ss = [w2k_load(wsrc_fn(0))]
            pend = None
            for m in range(8):
                if m + 1 < 8:
                    ss.append(w2k_load(wsrc_fn(m + 1)))
                s = ss[m]
                for seg in segs:
                    n, l0, si = seg["n"], seg["l0"], seg["si"]
                    b = bank()
                    mm_acc(b, 0, n, [(W2K[s][:, k, :], BIG[:, rhs_chunks[k], l0:l0 + n]) for k in range(8)],
                           [("w2k", s)] + [("B", rhs_chunks[k]) for k in range(8)])
                    if pend is not None:
                        stats_mm(*pend)
                    sqi = stats_sq(PB[b][:, 0:n], n, [("ps", b)], ser=[("psr", b)])
                    pend = (sqi, n, si, m == 0, m == 7)
                    P.op("dve", lambda e, m=m, b=b, n=n, l0=l0: e.tensor_copy(out=Fb[:, m, l0:l0 + n], in_=PB[b][:, 0:n]),
                         reads=[("ps", b), ("psr", b)], writes=kF(m))
            stats_mm(*pend)

        carry = {}

        def ffn(layer, which, p, pre=True, hoist=False):
            segs = SEGS[p]
            gpre = layer * 6 + (0 if which == 0 else 4)
            if pre:
                prenorm(gpre, segs[0:1], a2=True)
            if len(segs) > 1:
                prenorm(gpre, segs[1:2])

            def asrc(k, seg):
                if seg["n"] == 512:
                    return A2[:, k, 0:512], [("A2", k)]
                return Ab[:, k, seg["l0"]:seg["l0"] + seg["n"]], kA(k)
            wg, wu, wdn = wgate[layer, which], wup[layer, which], wdown[layer, which]
            pend = []

            def issue(j):
                pend.append((w2k_load(wg[:, j * 128:(j + 1) * 128]), w2k_load(wu[:, j * 128:(j + 1) * 128])))
            if "p0" in carry:
                pend.append(carry.pop("p0"))
            else:
                issue(0)
            dslots = [wd_load(wdn[:, 0:128])]
            for j in range(NJ):
                if j + 1 < NJ:
                    issue(j + 1)
                sg_, su_ = pend[j]
                for seg in segs:
                    n, l0 = seg["n"], seg["l0"]
                    if n == 512:
                        bg, bu, cg, cu = bank(), bank(), 0, 0
                    else:
                        bg = bu = NB
                        cg, cu = 0, 32
                    ards = [x for k in range(8) for x in asrc(k, seg)[1]]
                    mm_acc(bg, cg, n, [(W2K[sg_][:, k, :], asrc(k, seg)[0]) for k in range(8)], [("w2k", sg_)] + ards)
                    mm_acc(bu, cu, n, [(W2K[su_][:, k, :], asrc(k, seg)[0]) for k in range(8)], [("w2k", su_)] + ards)
                    i = nxt("sg", 2)
                    P.op("act", lambda e, i=i, bg=bg, cg=cg, n=n: e.activation(out=SG[i][:, 0:n], in_=PBX[bg][:, cg:cg + n], func=AF.Silu),
                         reads=[bk(bg)], writes=[("sg", i)])
                    P.op("dve", lambda e, i=i, j=j, bu=bu, cu=cu, n=n, l0=l0: e.tensor_tensor(
                        out=BIG[:, j, l0:l0 + n], in0=SG[i][:, 0:n], in1=PBX[bu][:, cu:cu + n], op=ALU.mult),
                        reads=[("sg", i), bk(bu)], writes=[("B", j)])
            if hoist:
                prenorm(gpre, SEGS[p + 1][0:1], a2=True)
                if not (layer == 2 and which == 0):
                    carry["p0"] = (w2k_load(wg[:, 0:128]), w2k_load(wu[:, 0:128]))
            pend2 = None
            for m in range(8):
                if m + 1 < 8:
                    dslots.append(wd_load(wdn[:, (m + 1) * 128:(m + 2) * 128]))
                sd = dslots[m]
                for seg in segs:
                    n, l0, si = seg["n"], seg["l0"], seg["si"]
                    b = bank()
                    mm_acc(b, 0, n, [(WD[sd][:, j, :], BIG[:, j, l0:l0 + n]) for j in range(11)],
                           [("wd", sd, 0)] + [("B", j) for j in range(11)], fin=False)
                    mm_acc(b, 0, n, [(WD[sd][:, j, :], BIG[:, j, l0:l0 + n]) for j in range(11, NJ)],
                           [("wd", sd, 1)] + [("B", j) for j in range(11, NJ)], first=False)
                    if pend2 is not None:
                        stats_mm(*pend2)
                    sqi = stats_sq(PB[b][:, 0:n], n, [("ps", b)], ser=[("psr", b)])
                    pend2 = (sqi, n, si, m == 0, m == 7)
                    P.op("dve", lambda e, m=m, b=b, n=n, l0=l0: e.tensor_copy(out=Fb[:, m, l0:l0 + n], in_=PB[b][:, 0:n]),
                         reads=[("ps", b), ("psr", b)], writes=kF(m))
            stats_mm(*pend2)
            postnorm(gpre + 1, segs)

        def pool_mixer(layer, p):
            segs = SEGS[p]
            prenorm(layer * 6 + 2, segs)
            Ukeys = [("U", m) for m in range(8)]
            if p == 0:
                P.op("pool", lambda e: e.memset(U[:, :, 0:16], 0.0), reads=Ukeys, writes=Ukeys)
                for c in range(8):
                    P.dma("sp", ("ld_us", c), lambda e, c=c: e.dma_start(
                        out=Us[:, c, :, 0:15], in_=spT[layer, c * 128:(c + 1) * 128, :].rearrange("p (b r) -> p b r", b=4)),
                        writes=[("Us", c)])
            else:
                P.op("pool", lambda e: e.tensor_copy(out=U[:, :, 0:16], in_=U[:, :, 512:528]), reads=Ukeys, writes=Ukeys)
            win = pwin[layer]
            ss = [w2k_load(win[:, 0:128])]
            for m in range(8):
                if m + 1 < 8:
                    ss.append(w2k_load(win[:, (m + 1) * 128:(m + 2) * 128]))
                s = ss[m]
                for seg in segs:
                    n, l0 = seg["n"], seg["l0"]
                    b = bank()
                    mm_acc(b, 0, n, [(W2K[s][:, k, :], Ab[:, k, l0:l0 + n]) for k in range(8)],
                           [("w2k", s)] + [x for k in range(8) for x in kA(k)])
                    if n == 512:
                        P.op("dve", lambda e, m=m, b=b: e.tensor_copy(out=U[:, m, 16:528], in_=PB[b][:, 0:512]),
                             reads=[("ps", b)], writes=[("U", m)])
                    else:
                        P.op("dve", lambda e, m=m, b=b: e.tensor_copy(
                            out=Us[:, m, :, 15:23], in_=PB[b][:, 0:32].rearrange("p (b t) -> p b t", b=4)),
                            reads=[("ps", b)], writes=[("Us", m)])
                if p == 3:
                    b = bank()
                    mm_acc(b, 0, 128, [(Ab[:, k, 497:544], W2K[s][:, k, :]) for k in range(8)],
                           [("w2k", s)] + [x for k in range(8) for x in kA(k)], M=47)
                    P.op("dve", lambda e, m=m, b=b: e.tensor_copy(out=TMP_tok[0:47, m * 128:(m + 1) * 128], in_=PB[b][0:47, 0:128]),
                         reads=[("ps", b)], writes=["TMPtok"])
            if p == 3:
                key = ("o", "pool_p", layer)
                P.dma("sp", ("pool_p", layer), lambda e: e.dma_start(out=pool_p[layer], in_=TMP_tok[0:15, :]), reads=["TMPtok"], writes=[key])
                out_keys.append(key)
                key = ("o", "pool_s", layer)
                P.dma("sp", ("pool_s", layer), lambda e: e.dma_start(out=pool_s[layer, :, 7:15, :], in_=TMP_tok[15:47, :]),
                      reads=["TMPtok"], writes=[key])
                out_keys.append(key)
            for m in range(8):
                gi = m // 2
                w = WINS[gi]
                for seg in segs:
                    if seg["n"] == 512:
                        src = U[:, m, :]
                        bufs = (PSa, PSb)
                        sh = 1
                        cur = src
                        for lv in range(gi + 1):
                            dst = bufs[lv % 2]
                            lo = 2 * sh - 1
                            P.op("pool", lambda e, dst=dst, cur=cur, lo=lo, sh=sh: e.tensor_tensor(
                                out=dst[:, lo:528], in0=cur[:, lo:528], in1=cur[:, lo - sh:528 - sh], op=ALU.add),
                                reads=[("U", m), ("psc", 0), ("psc", 1)], writes=[("psc", lv % 2)])
                            cur = dst
                            sh *= 2
                        P.op("dve", lambda e, m=m, cur=cur, w=w: e.scalar_tensor_tensor(
                            out=BIG[:, m, 0:512], in0=cur[:, 16:528], scalar=1.0 / w, in1=U[:, m, 16:528],
                            op0=ALU.mult, op1=ALU.subtract), reads=[("psc", gi % 2), ("U", m)], writes=[("B", m)])
                        if p == 0:
                            P.op("dve", lambda e, cur=cur, gi=gi: e.tensor_tensor(out=T16, in0=cur[:, 16:32], in1=RCT[:, gi, :], op=ALU.mult),
                                 reads=[("psc", gi % 2), "RCT"], writes=["T16"])
                            P.op("dve", lambda e, m=m: e.tensor_tensor(out=BIG[:, m, 0:16], in0=T16, in1=U[:, m, 16:32], op=ALU.subtract),
                                 reads=["T16", ("U", m)], writes=[("B", m)])
                    else:
                        src = Us[:, m, :, :]
                        bufs = (SSa, SSb)
                        sh = 1
                        cur = src
                        for lv in range(gi + 1):
                            dst = bufs[lv % 2]
                            lo = 2 * sh - 1
                            P.op("pool", lambda e, dst=dst, cur=cur, lo=lo, sh=sh: e.tensor_tensor(
                                out=dst[:, :, lo:23], in0=cur[:, :, lo:23], in1=cur[:, :, lo - sh:23 - sh], op=ALU.add),
                                reads=[("Us", m), ("ssc", 0), ("ssc", 1)], writes=[("ssc", lv % 2)])
                            cur = dst
                            sh *= 2
                        P.op("dve", lambda e, m=m, cur=cur, w=w: e.scalar_tensor_tensor(
                            out=BIG[:, m, 512:544].rearrange("p (b t) -> p b t", b=4), in0=cur[:, :, 15:23], scalar=1.0 / w,
                            in1=Us[:, m, :, 15:23], op0=ALU.mult, op1=ALU.subtract),
                            reads=[("ssc", gi % 2), ("Us", m)], writes=[("B", m)])
            sgw = wd_load(pwgrp[layer], kind="g")
            WG = WD[sgw][:].rearrange("p j m -> p (j m)")[:, 0:2048].rearrange("p (g k n) -> p g k n", g=4, k=2)
            for gi in range(4):
                for mo in range(2):
                    mch = 2 * gi + mo
                    for seg in segs:
                        n, l0 = seg["n"], seg["l0"]
                        b = bank()
                        mm_acc(b, 0, n, [(WG[:, gi, ki, mo * 128:(mo + 1) * 128], BIG[:, 2 * gi + ki, l0:l0 + n]) for ki in range(2)],
                               [("wd", sgw, 0), ("wd", sgw, 1), ("B", 2 * gi), ("B", 2 * gi + 1)])
                        P.op("dve", lambda e, mch=mch, b=b, n=n, l0=l0: e.tensor_scalar(
                            out=BIG[:, 8 + mch, l0:l0 + n], in0=PB[b][:, 0:n], scalar1=GV[:, mch, 25 + layer:26 + layer], scalar2=None,
                            op0=ALU.mult), reads=[("ps", b), "GV"], writes=[("B", 8 + mch)])
            wout = pwout[layer]
            out_proj(lambda m: wout[:, m * 128:(m + 1) * 128], [8 + k for k in range(8)], segs)
            postnorm(layer * 6 + 3, segs)

        def kv_phase(p):
            segs = SEGS[p]
            prenorm(24, segs)
            h0 = 512 * p
            areads = [x for k in range(8) for x in kA(k)]
            for which, wsrc, dst in (("k", wk_d, k_p), ("v", wv_d, v_p)):
                ss = [w2k_load(wsrc[:, 0:128])]
                for m in range(8):
                    if m + 1 < 8:
                        ss.append(w2k_load(wsrc[:, (m + 1) * 128:(m + 2) * 128]))
                    s = ss[m]
                    if which == "k":
                        b = bank()
                        mm_acc(b, 0, 512, [(W2K[s][:, k, :], Ab[:, k, 0:512]) for k in range(8)], [("w2k", s)] + areads)
                        P.op("act", lambda e, m=m, b=b, h0=h0: e.activation(out=KT[:, m, h0:h0 + 512], in_=PB[b][:, 0:512], func=AF.Copy),
                             reads=[("ps", b)], writes=[("KT", m, p)])
                        if p == 3:
                            b = bank()
                            mm_acc(b, 0, 32, [(W2K[s][:, k, :], Ab[:, k, 512:544]) for k in range(8)], [("w2k", s)] + areads)
                            P.op("act", lambda e, m=m, b=b: e.activation(out=KTs[:, m, :], in_=PB[b][:, 0:32], func=AF.Copy),
                                 reads=[("ps", b)], writes=["KTs"])
                    b = bank()
                    for tt in range(4):
                        mm_acc(b, tt * 128, 128, [(Ab[:, k, tt * 128:(tt + 1) * 128], W2K[s][:, k, :]) for k in range(8)],
                               [("w2k", s)] + areads)
                    i = nxt("ks", 2)
                    P.op("dve", lambda e, i=i, b=b: e.tensor_copy(out=KS[i][:].rearrange("p t x -> p (t x)"), in_=PB[b][:, 0:512]),
                         reads=[("ps", b)], writes=[("ks", i)])
                    key = ("o", which + "p", p, m)
                    P.dma("sp", ("ks", i), lambda e, i=i, m=m, dst=dst, h0=h0: e.dma_start(
                        out=dst[h0:h0 + 512, m * 128:(m + 1) * 128].rearrange("(t p) x -> p t x", p=128), in_=KS[i][:]),
                        reads=[("ks", i)], writes=[key])
                    out_keys.append(key)
                    if p == 3:
                        b = bank()
                        mm_acc(b, 0, 128, [(Ab[:, k, 512:544], W2K[s][:, k, :]) for k in range(8)], [("w2k", s)] + areads, M=32)
                        i = nxt("ks", 2)
                        P.op("dve", lambda e, i=i, b=b: e.tensor_copy(out=KS[i][0:32, 0, :], in_=PB[b][0:32, 0:128]),
                             reads=[("ps", b)], writes=[("ks", i)])
                        dsts = k_s if which == "k" else v_s
                        key = ("o", which + "s_new", m)
                        P.dma("sp", ("ks", i), lambda e, i=i, m=m, dsts=dsts: e.dma_start(
                            out=dsts[:, S - 8:S, m * 128:(m + 1) * 128], in_=KS[i][0:32, 0, :]), reads=[("ks", i)], writes=[key])
                        out_keys.append(key)
                        if which == "v":
                            P.op("act", lambda e, m=m, b=b: e.activation(
                                out=VsN[0:32, 256 * m:256 * m + 256].rearrange("p (j x) -> p j x", x=64)[:, 0:4:3, :],
                                in_=PB[b][0:32, 0:128].rearrange("p (j x) -> p j x", x=64), func=AF.Copy), reads=[("ps", b), ("ks", i)], writes=["VsN"])

        def vaug(vt, h, nk):
            hl = h % 8
            return vt[0:nk, 128 * hl:128 * hl + 128]

        def vs_load(src_rows, nk, reads, half):
            s = nxt("vs", NVS)
            for j in range(2):
                P.dma("pool", ("vs", s, j), lambda e, s=s, src_rows=src_rows, nk=nk, j=j, half=half: e.dma_start(
                    out=VS[s][0:nk, :].rearrange("p (c x) -> p c x", x=256)[:, :, 192 * j:192 * j + 64],
                    in_=src_rows[:, 512 * half:512 * half + 512].rearrange("p (c x) -> p c x", x=128)[:, :, 64 * j:64 * j + 64]),
                    reads=reads, writes=[("vsj", s, j)])
            return s

        def sl(start, cnt, step):
            return slice(start, start + step * (cnt - 1) + 1, step)

        def attn_units(p):
            h0 = 512 * p
            units = []
            for qb in range(4):
                T = h0 + 128 * qb
                blocks = []
                if T > 0:
                    blocks.append((sl(T - 128, 128, 1), sl(T - 128, 128, 1), 128, {(T - 128) // 512}))
                blocks.append((sl(T, 128, 1), sl(T, 128, 1), 128, {p}))
                units.append((0, 128, sl(128 * qb, 128, 1), blocks, (0 if T > 0 else 128)))
            for r in range(4):
                blocks = []
                if p > 0:
                    blocks.append((sl(h0 - 512 + r, 128, 4), sl(h0 - 512 + r, 128, 4), 128, {p - 1}))
                blocks.append((sl(h0 + r, 128, 4), sl(h0 + r, 128, 4), 128, {p}))
                units.append((1, 128, sl(r, 128, 4), blocks, (0 if p > 0 else 128)))
            nk = 32 * (p + 1)
            for r in range(16):
                blocks = [(sl(r, nk, 16), sl(r, nk, 16), nk, set(range(p + 1)))]
                units.append((2, 32, sl(r, 32, 16), blocks, 256 + 32 * p))
            return units

        def attn_mixer(layer, p):
            segs = SEGS[p]
            jl = layer - 2
            prenorm(layer * 6 + 2, segs)
            areads = [x for k in range(8) for x in kA(k)]
            wq = wq_d[jl]
            cols = [(c, g) for c in range(8) for g in range(3)]
            ss = [w2k_load(wq[:, cols[0][1] * D + cols[0][0] * 128:cols[0][1] * D + cols[0][0] * 128 + 128])]
            for i, (c, g) in enumerate(cols):
                if i + 1 < len(cols):
                    c2, g2 = cols[i + 1]
                    ss.append(w2k_load(wq[:, g2 * D + c2 * 128:g2 * D + c2 * 128 + 128]))
                s = ss[i]
                for seg in segs:
                    n, l0 = seg["n"], seg["l0"]
                    b = bank()
                    mm_acc(b, 0, n, [(W2K[s][:, k, :], Ab[:, k, l0:l0 + n]) for k in range(8)], [("w2k", s)] + areads)
                    P.op("act", lambda e, c=c, g=g, b=b, n=n, l0=l0: e.activation(out=BIG[:, 3 * c + g, l0:l0 + n], in_=PB[b][:, 0:n], func=AF.Copy),
                         reads=[("ps", b)], writes=[("B", 3 * c + g)])
            units = attn_units(p)
            LAG = 3
            for hf in range(2):
                items = [(ui, hl) for ui in range(len(units)) for hl in range(8)]
                uvs = {}

                def stage_a(ui, hl):
                    (g, nq, qsl, blocks, nscol) = units[ui]
                    if ui not in uvs:
                        uvs[ui] = [vs_load(v_p[vrows, :], nk, [("o", "vp", pp, m) for pp in kps for m in range(8)], hf)
                                   for (ksl, vrows, nk, kps) in blocks]
                    nb = len(blocks)
                    h = 8 * hf + hl
                    c, r0 = h // 2, 64 * (h % 2)
                    bs = bank()
                    for bi, (ksl, vrows, nk, kps) in enumerate(blocks):
                        mm_acc(bs, bi * nq, nq, [(KT[r0:r0 + 64, c, ksl], BIG[r0:r0 + 64, 3 * c + g, qsl])],
                               [("KT", c, pp) for pp in kps] + [("B", 3 * c + g)], M=nk)
                    nkm = blocks[0][2]
                    i = nxt("tt", 4)
                    cf = -slope(h) * DILS[g] / SCALE
                    P.op("dve", lambda e, i=i, bs=bs, nkm=nkm, w=nb * nq, nscol=nscol, cf=cf: e.scalar_tensor_tensor(
                        out=TT[i][0:nkm, 0:w], in0=NST[0:nkm, nscol:nscol + w], scalar=cf, in1=PB[bs][0:nkm, 0:w],
                        op0=ALU.mult, op1=ALU.add), reads=[("ps", bs), "NST"], writes=[("tt", i)])
                    P.op("act", lambda e, i=i, nkm=nkm, w=nb * nq: e.activation(out=PT[i][0:nkm, 0:w], in_=TT[i][0:nkm, 0:w], func=AF.Exp,
                                                                              scale=SCALE), reads=[("tt", i)], writes=[("pt", i)])
                    return i

                def stage_b(ui, hl, i):
                    (g, nq, qsl, blocks, nscol) = units[ui]
                    vsl = uvs[ui]
                    nb = len(blocks)
                    h = 8 * hf + hl
                    bo = bank()
                    mm_acc(bo, 0, nq, [(vaug(VS[vsl[bi]], h, blocks[bi][2]), PT[i][0:blocks[bi][2], bi * nq:(bi + 1) * nq]) for bi in range(nb)],
                           [("pt", i)] + [("vsj", s_, j_) for s_ in vsl for j_ in range(2)])
                    if g == 0:
                        P.op("dve", lambda e, hl=hl, bo=bo, qsl=qsl, nq=nq: e.tensor_copy(out=Fb[:, hl, qsl], in_=PB[bo][:, 0:nq]),
                             reads=[("ps", bo)], writes=kF(hl))
                    else:
                        P.op("dve", lambda e, hl=hl, bo=bo, qsl=qsl, nq=nq: e.tensor_tensor(
                            out=Fb[:, hl, qsl], in0=Fb[:, hl, qsl], in1=PB[bo][:, 0:nq], op=ALU.add),
                            reads=[("ps", bo)] + kF(hl), writes=kF(hl))

                st_i = {}
                for idx in range(len(items) + LAG):
                    if idx < len(items):
                        st_i[idx] = stage_a(*items[idx])
                    if idx - LAG >= 0:
                        stage_b(*items[idx - LAG], st_i.pop(idx - LAG))
                for hl in range(8):
                    h = 8 * hf + hl
                    c, r0 = h // 2, 64 * (h % 2)
                    b = bank()
                    P.op("pe", lambda e, b=b, hl=hl: e.matmul(PB[b][:, 0:512], lhsT=PSW[:], rhs=Fb[:, hl, 0:512], start=True, stop=True),
                         reads=kF(hl) + ["PSW"], writes=[("ps", b)])
                    P.op("act", lambda e, b=b, r0=r0: e.activation(out=LNB[r0:r0 + 64, :], in_=PB[b][r0:r0 + 64, 0:512], func=AF.Ln),
                         reads=[("ps", b)], writes=[("tmps", 0)])
                    P.op("act", lambda e, r0=r0: e.activation(out=REC[r0:r0 + 64, :], in_=LNB[r0:r0 + 64, :], func=AF.Exp, scale=-1.0),
                         reads=[("tmps", 0)], writes=[("tmps", 1)])
                    P.op("dve", lambda e, c=c, hl=hl, r0=r0: e.tensor_tensor(
                        out=BIG[r0:r0 + 64, 3 * c, 0:512], in0=Fb[r0:r0 + 64, hl, 0:512], in1=REC[r0:r0 + 64, :], op=ALU.mult),
                        reads=kF(hl) + [("tmps", 1)], writes=[("B", 3 * c)])
            if p == 3 and not DBG.get("nosample"):
                sample_attn()
            wo = wo_d[jl]
            out_proj(lambda m: wo[:, m * 128:(m + 1) * 128], [3 * k for k in range(8)], segs)
            postnorm(layer * 6 + 3, segs)

        def sample_attn():
            tiles = [(0, "c", 1920, 0)] + [(1, "c", 1536 + 128 * i, 1 + i) for i in range(4)] + [(2, "s", r, 5 + r) for r in range(8)]
            sa = DBG.get("sa", {})
            QZ = Fb[:].rearrange("p c n -> p (c n)")[:, 0:768].bitcast(BF16).rearrange("p (g h t) -> p g h t", g=3, h=16)
            qzk = kF(0) + kF(1)
            P.op("pool", lambda e: e.memset(QZ, 0.0), writes=qzk)
            for g in range(3):
                for hh in range(2):
                    r0 = 64 * hh
                    P.op("dve", lambda e, g=g, hh=hh, r0=r0: e.tensor_copy(out=QZ[r0:r0 + 64, g, hh:16:2, :], in_=BIG[r0:r0 + 64, g:24:3, 512:544]),
                         reads=[("B", 3 * c + g) for c in range(8)] + qzk, writes=qzk)
            for b in range(sa.get("nb", 4)):
                first = [True]
                qc = 512 + 8 * b

                def do_tile(g, nk, kT_fn, kreads, vt, vreads, tab_src):
                    i = nxt("sbi", 1)
                    P.dma("sp", ("sbi", i), lambda e, i=i, tab_src=tab_src, nk=nk: e.dma_start(out=SBI[i][0:nk, :], in_=tab_src),
                          writes=[("sbi", i)])
                    bs = bank()

                    def fs(e, bs=bs, nk=nk, g=g, kT_fn=kT_fn, b=b):
                        for h in range(16):
                            c, r0 = h // 2, 64 * (h % 2)
                            ins = e.matmul(PB[bs][0:nk, h * 8:(h + 1) * 8], lhsT=kT_fn(c, r0), rhs=QZ[:, g, h, 8 * b:8 * b + 8],
                                           start=(h == 0), stop=(h == 15), skip_group_check=True)
                        return ins
                    steps = sa.get("steps", 31)
                    if steps & 2:
                        P.op("pe", fs, reads=kreads + qzk, writes=[("ps", bs)])
                    ti = nxt("tt", 4)
                    if steps & 4:
                      P.op("dve", lambda e, ti=ti, bs=bs, nk=nk, i=i: e.scalar_tensor_tensor(
                        out=TT[ti][0:nk, 0:128], in0=PB[bs][0:nk, 0:128], scalar=SCALE, in1=SBI[i][0:nk, :], op0=ALU.mult, op1=ALU.add),
                        reads=[("ps", bs), ("sbi", i)], writes=[("tt", ti)])
                    if steps & 8:
                      P.op("act", lambda e, ti=ti, nk=nk: e.activation(out=PT[ti][0:nk, 0:128], in_=TT[ti][0:nk, 0:128], func=AF.Exp),
                         reads=[("tt", ti)], writes=[("pt", ti)])
                    fl = first[0]
                    first[0] = False

                    def fo(e, nk=nk, vt=vt, ti=ti, fl=fl):
                        for h in range(16):
                            ins = e.matmul(PST[1][:, h * 8:(h + 1) * 8], lhsT=vaug(vt[h // 8], h, nk), rhs=PT[ti][0:nk, h * 8:(h + 1) * 8],
                                           start=(fl and h == 0), stop=False, skip_group_check=True)
                        return ins
                    if steps & 16:
                        P.op("pe", fo, reads=[("pt", ti)] + vreads, writes=[("pst", 1)])

                for (g, kind, r, tab) in (tiles if sa.get("cache", True) else []):
                    if kind not in sa.get("kinds", "cs"):
                        continue
                    rows = sl(r, 128, 1) if kind == "c" else sl(r, 128, 16)
                    ki = nxt("kr", 2)
                    P.dma("pool", ("kr", ki), lambda e, ki=ki, rows=rows, b=b: e.dma_start(out=KR[ki], in_=ck[b, rows, :]), writes=[("sg", ki)])

                    def ftr(e, ki=ki):
                        for c in range(8):
                            ins = e.transpose(PTR[:, c * 128:(c + 1) * 128], KR[ki][:, c * 128:(c + 1) * 128], IDN[:])
                        return ins
                    P.op("pe", ftr, reads=[("sg", ki), "IDN"], writes=["ptr"])
                    P.op("act", lambda e, ki=ki: e.activation(out=KsT[ki].rearrange("p c n -> p (c n)"), in_=PTR[:, :], func=AF.Copy),
                         reads=["ptr"], writes=[("ks", ki)])
                    vsa = vs_load(cv[b, rows, :], 128, [], 0)
                    vsb = vs_load(cv[b, rows, :], 128, [], 1)
                    do_tile(g, 128, lambda c, r0, ki=ki: KsT[ki][:, c, :], [("ks", ki)], (VS[vsa], VS[vsb]),
                            [("vsj", v_, j_) for v_ in (vsa, vsb) for j_ in range(2)], sbias_d[tab])
                for g in range(3 if sa.get("new", True) else 0):
                    do_tile(g, 32, lambda c, r0: KTs[:, c, :], ["KTs"], (VsN[:, 0:1024], VsN[:, 1024:2048]), ["VsN"], sbias_d[13 + 4 * g + b, 0:32, :])
                if not sa.get("norm", True):
                    continue
                P.op("dve", lambda e: e.tensor_copy(out=S1[:], in_=PST[1][:, 0:128]), reads=[("pst", 1)], writes=["S1"])
                b2 = bank()
                P.op("pe", lambda e, b2=b2: e.matmul(PB[b2][:, 0:128], lhsT=PSW[:], rhs=S1[:], start=True, stop=True),
                     reads=["S1", "PSW"], writes=[("ps", b2)])
                for hh in range(2):
                    r0 = 64 * hh
                    pv = PB[b2][r0:r0 + 64, 0:128].rearrange("p (c x t) -> p c x t", c=8, x=2)[:, :, hh, :]
                    lv = LNB[r0:r0 + 64, 0:64].rearrange("p (c t) -> p c t", c=8)
                    rv = REC[r0:r0 + 64, 0:64].rearrange("p (c t) -> p c t", c=8)
                    sv = S1[r0:r0 + 64, :].rearrange("p (c x t) -> p c x t", c=8, x=2)[:, :, hh, :]
                    P.op("act", lambda e, pv=pv, lv=lv: e.activation(out=lv, in_=pv, func=AF.Ln), reads=[("ps", b2)], writes=[("tmps", 0)])
                    P.op("act", lambda e, lv=lv, rv=rv: e.activation(out=rv, in_=lv, func=AF.Exp, scale=-1.0), reads=[("tmps", 0)], writes=[("tmps", 1)])
                    P.op("dve", lambda e, sv=sv, rv=rv, r0=r0, qc=qc: e.tensor_tensor(
                        out=BIG[r0:r0 + 64, 0:24:3, qc:qc + 8], in0=sv, in1=rv, op=ALU.mult),
                        reads=["S1", ("tmps", 1)], writes=[("B", 3 * c) for c in range(8)])

        def body():
            for layer in range(4):
                if layer == 2:
                    P.barrier()
                for p in range(4):
                    if layer == 2:
                        kv_phase(p)
                    ffn(layer, 0, p, pre=(p == 0), hoist=(p < 3))
                    if stop == ("ffn0", layer, p):
                        return
                for p in range(4):
                    if layer < 2:
                        pool_mixer(layer, p)
                    else:
                        attn_mixer(layer, p)
                    if stop == ("mix", layer, p):
                        return
                for p in range(4):
                    ffn(layer, 1, p, pre=(p == 0), hoist=(p < 3))
                    if stop == ("ffn1", layer, p):
                        return
        body()
        key = ("o", "yT")
        P.dma("sp", "st_y", lambda e: e.dma_start(out=yT.rearrange("(c p) n -> p c n", p=128), in_=H[:]),
              reads=[("h", c, sid) for c in range(8) for sid in (0, 1, 2, 3, "s")], writes=[key])
        out_keys.append(key)
        P.op("sp", None, reads=out_keys)
        P.emit()
    return nc


def host_tables():
    ident = np.eye(128, dtype=np.float32)
    psw = np.zeros((128, 128), np.float32)
    for i in range(128):
        psw[i, (i + 64) % 128] = 1.0
    k = np.arange(128)[:, None]
    q = np.arange(128)[None, :]
    ns_prev = np.where(k >= q, q + 128 - k, BIGN).astype(np.float32)
    ns_cur = np.where(q >= k, q - k, BIGN).astype(np.float32)
    ns2 = np.zeros((128, 128), np.float32)
    for p in range(4):
        j = np.arange(32)[None, :]
        st = 32 * p + j - k
        ns2[:, 32 * p:32 * p + 32] = np.where(st >= 0, st, BIGN)
    nstab = np.concatenate([ns_prev, ns_cur, ns2], axis=1).astype(np.float32)
    rct = np.zeros((128, 4, 16), np.float32)
    for gi, w in enumerate(WINS):
        rct[:, gi, :] = 1.0 / np.minimum(np.arange(16) + 1, w)
    rctab = rct.reshape(128, 64)
    sl_h = np.array([slope(h) for h in range(16)], np.float64)
    t = np.arange(8)
    sbias = np.full((25, 128, 128), NEGB, np.float64)

    def fill(tab, rows, g, valid_extra=None):
        dil = DILS[g]
        dist = 2048 + t[None, :] - rows[:, None]
        ok = (dist >= 0) & (dist % dil == 0) & (dist // dil <= 128)
        if valid_extra is not None:
            ok = ok & valid_extra
        for h in range(16):
            vals = np.where(ok, -sl_h[h] * dist, NEGB)
            sbias[tab, :rows.shape[0], h * 8:(h + 1) * 8] = vals
    fill(0, 1920 + np.arange(128), 0)
    for i in range(4):
        fill(1 + i, 1536 + 128 * i + np.arange(128), 1)
    for r in range(8):
        fill(5 + r, r + 16 * np.arange(128), 2)
    for g in range(3):
        for b in range(4):
            pidx = np.arange(32)
            rows = 2048 + (pidx % 8)
            ok = ((pidx // 8) == b)[:, None] & np.ones((1, 8), bool)
            fill(13 + 4 * g + b, rows, g, ok)
    return ident, psw, nstab, rctab, sbias.astype(np.float32)


_CACHE = {}


def kernel(x_prompt, x_sample, state_pool, cache_k, cache_v, norm_g, ffn_w_gate, ffn_w_up, ffn_w_down,
           pool_w_in, pool_w_grp, pool_scale, pool_w_out, kv_norm, w_k, w_v, attn_w_q, attn_w_o):
    f32 = np.float32
    a = lambda v: np.ascontiguousarray(np.asarray(v, dtype=f32))
    x_prompt, x_sample, state_pool, cache_k, cache_v = map(a, (x_prompt, x_sample, state_pool, cache_k, cache_v))
    if "nc" not in _CACHE:
        _CACHE["nc"] = build_program()
    nc = _CACHE["nc"]
    ident, psw, nstab, rctab, sbias = host_tables()
    vecs = np.concatenate([a(norm_g).reshape(24, D), a(kv_norm).reshape(1, D), a(pool_scale).reshape(2, D)], axis=0)
    gvec = np.ascontiguousarray(vecs.reshape(NVEC, 8, 128).transpose(2, 1, 0)).reshape(128, 8 * NVEC)
    shared = {
        "gvec": gvec, "wgate": a(ffn_w_gate), "wup": a(ffn_w_up), "wdown": a(ffn_w_down), "pwin": a(pool_w_in),
        "pwgrp": a(pool_w_grp), "pwout": a(pool_w_out), "wk": a(w_k), "wv": a(w_v), "wq": a(attn_w_q), "wo": a(attn_w_o),
        "ident": ident, "psw": psw, "nstab": nstab, "rctab": rctab, "sbias": sbias,
    }
    in_maps = []
    for i in range(8):
        xs = x_sample[4 * i:4 * i + 4].reshape(NSM, D)
        xT = np.ascontiguousarray(np.concatenate([x_prompt[i], xs], axis=0).T)
        sp = state_pool[:, 4 * i:4 * i + 4]
        spT = np.ascontiguousarray(sp.transpose(0, 3, 1, 2)).reshape(2, D, 60)
        sprow = np.ascontiguousarray(sp[:, :, 8:15, :])
        m = dict(shared)
        m.update({"xT": xT, "spT": spT, "sprow": sprow,
                  "ck": np.ascontiguousarray(cache_k[4 * i:4 * i + 4].reshape(4, S, D)),
                  "cv": np.ascontiguousarray(cache_v[4 * i:4 * i + 4].reshape(4, S, D))})
        in_maps.append(m)
    res = run_bass_kernel_spmd(nc, in_maps, core_ids=list(range(8)))
    R = res.results
    y_prompt = np.stack([R[i]["yT"][:, :S].T for i in range(8)], axis=0)
    y_sample = np.concatenate([R[i]["yT"][:, S:].T.reshape(4, 8, D) for i in range(8)], axis=0)
    pool_p = np.stack([R[i]["pool_p"] for i in range(8)], axis=1)
    k_p = np.stack([R[i]["k_p"].reshape(S, 16, 64) for i in range(8)], axis=0)
    v_p = np.stack([R[i]["v_p"].reshape(S, 16, 64) for i in range(8)], axis=0)
    pool_s = np.concatenate([R[i]["pool_s"] for i in range(8)], axis=1)
    k_s = np.concatenate([R[i]["k_s"].reshape(4, S, 16, 64) for i in range(8)], axis=0)
    v_s = np.concatenate([R[i]["v_s"].reshape(4, S, 16, 64) for i in range(8)], axis=0)
    c = lambda v: np.ascontiguousarray(v, dtype=f32)
    return (c(y_prompt), c(y_sample), c(pool_p), c(k_p), c(v_p), c(pool_s), c(k_s), c(v_s))
```

```python
import contextlib
import numpy as np
import concourse.bass as bass
import concourse.mybir as mybir
from concourse.bass_utils import run_bass_kernel_spmd

F32 = mybir.dt.float32
BF16 = mybir.dt.bfloat16
AF = mybir.ActivationFunctionType
ALU = mybir.AluOpType

D = 1024
NJ = 22
S = 2048
NSM = 32
HT = S + NSM
NTOK = 544
SCALE = 0.125
EPS = 1e-6
BIGN = 1.0e5
NEGB = -30000.0
WINS = (2, 4, 8, 16)
DILS = (1, 4, 16)
NVEC = 27


class Prog:
    ENG = ("pe", "act", "dve", "pool", "sp")

    def __init__(self, nc):
        self.nc = nc
        self.ops = []
        self.last_w = {}
        self.readers = {}

    def op(self, eng, fn, reads=(), writes=(), lane=None, extra=()):
        deps = set(extra)
        for k in reads:
            w = self.last_w.get(k)
            if w is not None:
                deps.add(w)
        for k in writes:
            w = self.last_w.get(k)
            if w is not None:
                deps.add(w)
            rd = self.readers.get(k)
            if rd:
                deps.update(rd.values())
        idx = len(self.ops)
        self.ops.append([eng, fn, deps, lane, False, 0])
        rkey = (eng, idx) if lane is not None else eng
        for k in reads:
            self.readers.setdefault(k, {})[rkey] = idx
        for k in writes:
            self.last_w[k] = idx
            self.readers[k] = {}
        return idx

    def dma(self, queue, lane, fn, reads=(), writes=()):
        return self.op(queue, fn, reads, writes, lane=lane)

    def barrier(self):
        last = {}
        for i, o in enumerate(self.ops):
            if o[1] is None:
                continue
            if o[3] is not None:
                last[("l", o[3])] = i
            else:
                last[("e", o[0])] = i
        deps = set(last.values())
        for e in self.ENG:
            self.op(e, None, extra=deps)

    def emit(self):
        nc = self.nc
        ops = self.ops
        for o in ops:
            for d in o[2]:
                dd = ops[d]
                if dd[3] is None and dd[0] == "pe" and o[0] == "pe" and o[3] is None:
                    continue
                dd[4] = True
        cnt = {e: 0 for e in self.ENG}
        lane_cnt = {}
        for o in ops:
            if o[3] is not None:
                lane_cnt[o[3]] = lane_cnt.get(o[3], 0) + 16
                o[5] = lane_cnt[o[3]]
                o[4] = True
            elif o[4]:
                cnt[o[0]] += 1
                o[5] = cnt[o[0]]
        lane_names = sorted(lane_cnt.keys(), key=str)
        with contextlib.ExitStack() as st:
            sems = {e: st.enter_context(nc.semaphore("s_" + e)) for e in self.ENG if e != "sp"}
            lsem = {}
            for i, ln in enumerate(lane_names):
                lsem[ln] = st.enter_context(nc.semaphore("l%d" % i))
            block = st.enter_context(nc.Block())
            per_eng = {e: [] for e in self.ENG}
            for i, o in enumerate(ops):
                per_eng[o[0]].append(i)

            def run(engname, engobj):
                waited = {}
                for i in per_eng[engname]:
                    eng, fn, deps, lane, sig, val = ops[i]
                    need = {}
                    for d in deps:
                        dd = ops[d]
                        if dd[3] is not None:
                            key = ("l", dd[3])
                        else:
                            if dd[0] == "pe" and engname == "pe" and lane is None:
                                continue
                            key = ("e", dd[0])
                        if dd[5] > need.get(key, 0):
                            need[key] = dd[5]
                    for key, v in need.items():
                        if waited.get(key, 0) >= v:
                            continue
                        waited[key] = v
                        s = lsem[key[1]] if key[0] == "l" else sems[key[1]]
                        engobj.wait_ge(s, v)
                    if fn is None:
                        continue
                    ins = fn(engobj)
                    if lane is not None:
                        ins.then_inc(lsem[lane], 16)
                    elif sig:
                        ins.then_inc(sems[engname], 1)

            @block.tensor
            def _(e):
                run("pe", e)

            @block.scalar
            def _(e):
                run("act", e)

            @block.vector
            def _(e):
                run("dve", e)

            @block.gpsimd
            def _(e):
                run("pool", e)

            @block.sync
            def _(e):
                run("sp", e)


DBG = {}


def slope(h):
    return 2.0 ** (-(h + 1) / 2.0)


def build_program(stop=None):
    nc = bass.Bass("TRN2", target_bir_lowering=False)

    def din(name, shape):
        return nc.dram_tensor(name, list(shape), F32, kind="ExternalInput").ap()

    def dout(name, shape):
        return nc.dram_tensor(name, list(shape), F32, kind="ExternalOutput").ap()

    xT = din("xT", [D, HT])
    spT = din("spT", [2, D, 60])
    sprow = din("sprow", [2, 4, 7, D])
    ck = din("ck", [4, S, D])
    cv = din("cv", [4, S, D])
    gvec_d = din("gvec", [128, 8 * NVEC])
    wgate = din("wgate", [4, 2, D, 2816])
    wup = din("wup", [4, 2, D, 2816])
    wdown = din("wdown", [4, 2, 2816, D])
    pwin = din("pwin", [2, D, D])
    pwgrp = din("pwgrp", [2, 4, 256, 256])
    pwout = din("pwout", [2, D, D])
    wk_d = din("wk", [D, D])
    wv_d = din("wv", [D, D])
    wq_d = din("wq", [2, D, 3 * D])
    wo_d = din("wo", [2, D, D])
    ident_d = din("ident", [128, 128])
    psw_d = din("psw", [128, 128])
    nstab_d = din("nstab", [128, 384])
    rctab_d = din("rctab", [128, 64])
    sbias_d = din("sbias", [25, 128, 128])

    yT = dout("yT", [D, HT])
    pool_p = dout("pool_p", [2, 15, D])
    k_p = dout("k_p", [S, D])
    v_p = dout("v_p", [S, D])
    pool_s = dout("pool_s", [2, 4, 15, D])
    k_s = dout("k_s", [4, S, D])
    v_s = dout("v_s", [4, S, D])

    st = contextlib.ExitStack()
    with st:
        def sb(name, shape, dt):
            return st.enter_context(nc.sbuf_tensor(name, list(shape), dt))

        def psum(name, shape, dt):
            return st.enter_context(nc.psum_tensor(name, list(shape), dt))

        H = sb("H", [128, 8, HT], F32)
        Fb = sb("Fb", [128, 8, NTOK], F32)
        Ab = Fb[:].rearrange("p c n -> p (c n)")[:, 0:4 * NTOK].bitcast(BF16).rearrange("p (c n) -> p c n", c=8)
        BIG = sb("BIG", [128, 24, NTOK], BF16)
        KTr = sb("KTr", [128, 8 * S], BF16)
        KT = KTr[:].rearrange("p (c n) -> p c n", c=8)
        KF = KTr[:].bitcast(F32)
        U = KF[:, 0:8 * 528].rearrange("p (c n) -> p c n", c=8)
        Us = KF[:, 4224:4224 + 8 * 92].rearrange("p (c b n) -> p c b n", c=8, b=4)
        PSa = KF[:, 4960:4960 + 528]
        PSb = KF[:, 5488:5488 + 528]
        SSa = KF[:, 6016:6016 + 92].rearrange("p (b n) -> p b n", b=4)
        SSb = KF[:, 6108:6108 + 92].rearrange("p (b n) -> p b n", b=4)
        T16 = KF[:, 6200:6216]
        TMP_tok = KF[:, 6216:6216 + 1024]
        NW2K = 5
        W2K = [sb("w2k%d" % i, [128, 8, 128], BF16) for i in range(NW2K)]
        WD = [sb("wd%d" % i, [128, NJ, 128], BF16) for i in range(2)]
        NVS = 4
        VS = [sb("vs%d" % i, [128, 1024], BF16) for i in range(NVS)]
        VsN = sb("VsN", [32, 2048], BF16)
        KTs = sb("KTs", [128, 8, 32], BF16)
        GV = sb("GV", [128, 8, NVEC], F32)
        ONES = sb("ONES", [128, 128], BF16)
        IDN = sb("IDN", [128, 128], BF16)
        PSW = sb("PSW", [128, 128], F32)
        NST = sb("NST", [128, 384], F32)
        RCT = sb("RCT", [128, 4, 16], F32)
        EPSV = sb("EPSV", [128, 1], F32)
        SQ = [sb("sq%d" % i, [128, 512], BF16) for i in range(2)]
        TMPS = [sb("tmps%d" % i, [128, 512], F32) for i in range(2)]
        RSTD = [sb("rstd0", [128, 512], F32), sb("rstd1", [128, NSM], F32)]
        A2 = sb("A2", [128, 8, 512], BF16)
        SG = [sb("sg%d" % i, [128, 512], F32) for i in range(2)]
        TT = [sb("tt%d" % i, [128, 256], F32) for i in range(4)]
        PT = [sb("pt%d" % i, [128, 256], BF16) for i in range(4)]
        KS = [sb("ks%d" % i, [128, 4, 128], F32) for i in range(2)]
        KR = [SG[i][:].bitcast(BF16) for i in range(2)]
        KsT = [KS[i][:].rearrange("p t x -> p (t x)").bitcast(BF16).rearrange("p (c n) -> p c n", c=8) for i in range(2)]
        SBI = [sb("sbi%d" % i, [128, 128], F32) for i in range(1)]
        S1 = sb("S1", [128, 128], F32)
        LNB = TMPS[0]
        REC = TMPS[1]

        NB = 5
        PB = [psum("pb%d" % i, [128, 512], F32) for i in range(NB)]
        PST = [psum("pst%d" % i, [128, 512], F32) for i in range(2)]
        PTR = psum("ptr", [128, D], BF16)
        PBX = PB + [PST[1]]

        def bk(b):
            return ("ps", b) if b < NB else ("pst", 1)

        P = Prog(nc)
        ctr = {"bank": 0, "w2k": 0, "wd": 0, "vs": 0, "sq": 0, "sg": 0, "tt": 0, "ks": 0, "kr": 0, "sbi": 0}

        def nxt(name, n):
            v = ctr[name]
            ctr[name] = (v + 1) % n
            return v

        def bank():
            return nxt("bank", NB)

        def kA(k):
            return [("A", k), ("F", k // 2)]

        def kF(m):
            return [("F", m)] + ([("A", 2 * m), ("A", 2 * m + 1)] if m < 4 else [])

        SEGS = []
        for p in range(4):
            sg = [dict(h0=512 * p, n=512, l0=0, id=p, si=0)]
            if p == 3:
                sg.append(dict(h0=S, n=NSM, l0=512, id="s", si=1))
            SEGS.append(sg)

        P.dma("sp", "ld_h", lambda e: e.dma_start(out=H[:], in_=xT.rearrange("(c p) n -> p c n", p=128)),
              writes=[("h", c, sid) for c in range(8) for sid in (0, 1, 2, 3, "s")])
        P.dma("sp", "ld_gv", lambda e: e.dma_start(out=GV[:].rearrange("p c v -> p (c v)"), in_=gvec_d), writes=["GV"])
        P.dma("sp", "ld_psw", lambda e: e.dma_start(out=PSW[:], in_=psw_d), writes=["PSW"])
        P.dma("sp", "ld_nst", lambda e: e.dma_start(out=NST[:], in_=nstab_d), writes=["NST"])
        P.dma("sp", "ld_rct", lambda e: e.dma_start(out=RCT[:].rearrange("p g n -> p (g n)"), in_=rctab_d), writes=["RCT"])
        P.dma("pool", "ld_idn", lambda e: e.dma_start(out=IDN[:], in_=ident_d), writes=["IDN"])
        out_keys = []
        for b in range(0 if not DBG.get("nocc") else 0, 4 if not DBG.get("nocc") else 0):
            for nm, src, dst in (("k", ck, k_s), ("v", cv, v_s)):
                key = ("o", "cc", nm, b)
                P.dma("sp", ("cc", nm, b), lambda e, src=src, dst=dst, b=b: e.dma_start(
                    out=dst[b, 0:S - 8, :].rearrange("r d -> (r d)").rearrange("(a x) -> a x", a=128),
                    in_=src[b, 8:S, :].rearrange("r d -> (r d)").rearrange("(a x) -> a x", a=128)),
                      writes=[key])
                out_keys.append(key)
        for l in range(2 if not DBG.get("nosprow") else 0):
            key = ("o", "sprow", l)
            P.dma("sp", ("sprow", l), lambda e, l=l: e.dma_start(out=pool_s[l, :, 0:7, :], in_=sprow[l]), writes=[key])
            out_keys.append(key)
        P.op("dve", lambda e: e.memset(ONES[:], 1.0), writes=["ONES"])
        P.op("dve", lambda e: e.memset(EPSV[:], EPS), writes=["EPSV"])
        for i in range(NVS):
            P.op("pool", lambda e, i=i: e.memset(VS[i][:].rearrange("p (c x) -> p c x", x=256)[:, :, 64:192], 1.0), writes=[("vsj", i, 0), ("vsj", i, 1)])
        P.op("pool", lambda e: e.memset(VsN[:].rearrange("p (c x) -> p c x", x=256)[:, :, 64:192], 1.0), writes=["VsN"])
        for l in range(4):
            for idx in (l * 6 + 1, l * 6 + 5):
                P.op("dve", lambda e, idx=idx: e.tensor_scalar(out=GV[:, :, idx:idx + 1], in0=GV[:, :, idx:idx + 1], scalar1=0.5,
                                                               scalar2=None, op0=ALU.mult), reads=["GV"], writes=["GV"])

        def w2k_load(src):
            s = nxt("w2k", NW2K)
            P.dma("pool", ("w2k", s), lambda e, s=s, src=src: e.dma_start(out=W2K[s][:], in_=src.rearrange("(c p) m -> p c m", p=128)),
                  writes=[("w2k", s)])
            return s

        def wd_load(src, kind="d"):
            s = nxt("wd", 2)
            if kind == "d":
                for hf in range(2):
                    P.dma("pool", ("wd", s, hf), lambda e, s=s, src=src, hf=hf: e.dma_start(
                        out=WD[s][:, 11 * hf:11 * hf + 11, :],
                        in_=src[1408 * hf:1408 * hf + 1408, :].rearrange("(j p) m -> p j m", p=128)), writes=[("wd", s, hf)])
            else:
                P.dma("pool", ("wd", s, 0), lambda e, s=s, src=src: e.dma_start(
                    out=WD[s][:].rearrange("p j m -> p (j m)")[:, 0:2048].rearrange("p (g k n) -> p g k n", g=4, k=2),
                    in_=src.rearrange("g (k p) n -> p g k n", p=128)), writes=[("wd", s, 0), ("wd", s, 1)])
            return s

        def mm_acc(b, col, n, pairs, reads, M=128, extra_writes=(), first=True, fin=True):
            def f(e, b=b, col=col, n=n, pairs=pairs, M=M, first=first, fin=fin):
                last = len(pairs) - 1
                for i, (l, r) in enumerate(pairs):
                    ins = e.matmul(PBX[b][0:M, col:col + n], lhsT=l, rhs=r, start=(first and i == 0), stop=(fin and i == last))
                return ins
            return P.op("pe", f, reads=reads, writes=[bk(b)] + list(extra_writes))

        def stats_add(src_ap, n, si, first, last, reads, ser=()):
            i = nxt("sq", 2)
            P.op("act", lambda e, i=i, src_ap=src_ap, n=n: e.activation(out=SQ[i][:, 0:n], in_=src_ap, func=AF.Square),
                 reads=reads, writes=[("sq", i)] + list(ser))
            P.op("pe", lambda e, i=i, n=n, si=si, first=first, last=last: e.matmul(
                PST[si][:, 0:n], lhsT=ONES[:], rhs=SQ[i][:, 0:n], start=first, stop=last),
                reads=[("sq", i), "ONES"], writes=[("pst", si)])

        def stats_sq(src_ap, n, reads, ser=()):
            i = nxt("sq", 2)
            P.op("act", lambda e, i=i, src_ap=src_ap, n=n: e.activation(out=SQ[i][:, 0:n], in_=src_ap, func=AF.Square),
                 reads=reads, writes=[("sq", i)] + list(ser))
            return i

        def stats_mm(i, n, si, first, last):
            P.op("pe", lambda e, i=i, n=n, si=si, first=first, last=last: e.matmul(
                PST[si][:, 0:n], lhsT=ONES[:], rhs=SQ[i][:, 0:n], start=first, stop=last),
                reads=[("sq", i), "ONES"], writes=[("pst", si)])

        def stats_finish(n, si):
            P.op("act", lambda e, n=n, si=si: e.activation(out=TMPS[si][:, 0:n], in_=PST[si][:, 0:n], func=AF.Ln, scale=1.0 / D, bias=EPSV[:, 0:1]),
                 reads=[("pst", si), "EPSV"], writes=[("tmps", si)])
            P.op("act", lambda e, n=n, si=si: e.activation(out=RSTD[si][:, 0:n], in_=TMPS[si][:, 0:n], func=AF.Exp, scale=-0.5),
                 reads=[("tmps", si)], writes=[("rstd", si)])

        def prenorm(gidx, segs, a2=False):
            for seg in segs:
                h0, n, l0, sid, si = seg["h0"], seg["n"], seg["l0"], seg["id"], seg["si"]
                for c in range(8):
                    stats_add(H[:, c, h0:h0 + n], n, si, c == 0, c == 7, [("h", c, sid)])
                stats_finish(n, si)
                for c in range(8):
                    dst = A2[:, c, 0:n] if a2 else Ab[:, c, l0:l0 + n]
                    wk = [("A2", c)] if a2 else kA(c)
                    P.op("dve", lambda e, c=c, h0=h0, n=n, si=si, dst=dst: e.scalar_tensor_tensor(
                        out=dst, in0=H[:, c, h0:h0 + n], scalar=GV[:, c, gidx:gidx + 1], in1=RSTD[si][:, 0:n],
                        op0=ALU.mult, op1=ALU.mult), reads=[("h", c, sid), ("rstd", si), "GV"], writes=wk)

        def postnorm(gidx, segs):
            for seg in segs:
                h0, n, l0, sid, si = seg["h0"], seg["n"], seg["l0"], seg["id"], seg["si"]
                stats_finish(n, si)
                for m in range(8):
                    P.op("dve", lambda e, m=m, n=n, l0=l0, si=si: e.scalar_tensor_tensor(
                        out=Fb[:, m, l0:l0 + n], in0=Fb[:, m, l0:l0 + n], scalar=GV[:, m, gidx:gidx + 1], in1=RSTD[si][:, 0:n],
                        op0=ALU.mult, op1=ALU.mult), reads=kF(m) + [("rstd", si), "GV"], writes=kF(m))
                    P.op("dve", lambda e, m=m, h0=h0, n=n, l0=l0: e.tensor_tensor(
                        out=H[:, m, h0:h0 + n], in0=H[:, m, h0:h0 + n], in1=Fb[:, m, l0:l0 + n], op=ALU.add),
                        reads=kF(m) + [("h", m, sid)], writes=[("h", m, sid)])

        def out_proj(wsrc_fn, rhs_chunks, segs):
            ss = [w2k_load(wsrc_fn(0))]
            pend = None
            for m in range(8):
                if m + 1 < 8:
                    ss.append(w2k_load(wsrc_fn(m + 1)))
                s = ss[m]
                for seg in segs:
                    n, l0, si = seg["n"], seg["l0"], seg["si"]
                    b = bank()
                    mm_acc(b, 0, n, [(W2K[s][:, k, :], BIG[:, rhs_chunks[k], l0:l0 + n]) for k in range(8)],
                           [("w2k", s)] + [("B", rhs_chunks[k]) for k in range(8)])
                    if pend is not None:
                        stats_mm(*pend)
                    sqi = stats_sq(PB[b][:, 0:n], n, [("ps", b)], ser=[("psr", b)])
                    pend = (sqi, n, si, m == 0, m == 7)
                    P.op("dve", lambda e, m=m, b=b, n=n, l0=l0: e.tensor_copy(out=Fb[:, m, l0:l0 + n], in_=PB[b][:, 0:n]),
                         reads=[("ps", b), ("psr", b)], writes=kF(m))
            stats_mm(*pend)

        carry = {}

        def ffn(layer, which, p, pre=True, hoist=False):
            segs = SEGS[p]
            gpre = layer * 6 + (0 if which == 0 else 4)
            if pre:
                prenorm(gpre, segs[0:1], a2=True)
            if len(segs) > 1:
                prenorm(gpre, segs[1:2])

            def asrc(k, seg):
                if seg["n"] == 512:
                    return A2[:, k, 0:512], [("A2", k)]
                return Ab[:, k, seg["l0"]:seg["l0"] + seg["n"]], kA(k)
            wg, wu, wdn = wgate[layer, which], wup[layer, which], wdown[layer, which]
            pend = []

            def issue(j):
                pend.append((w2k_load(wg[:, j * 128:(j + 1) * 128]), w2k_load(wu[:, j * 128:(j + 1) * 128])))
            if "p0" in carry:
                pend.append(carry.pop("p0"))
            else:
                issue(0)
            dslots = [wd_load(wdn[:, 0:128])]
            for j in range(NJ):
                if j + 1 < NJ:
                    issue(j + 1)
                sg_, su_ = pend[j]
                for seg in segs:
                    n, l0 = seg["n"], seg["l0"]
                    if n == 512:
                        bg, bu, cg, cu = bank(), bank(), 0, 0
                    else:
                        bg = bu = NB
                        cg, cu = 0, 32
                    ards = [x for k in range(8) for x in asrc(k, seg)[1]]
                    mm_acc(bg, cg, n, [(W2K[sg_][:, k, :], asrc(k, seg)[0]) for k in range(8)], [("w2k", sg_)] + ards)
                    mm_acc(bu, cu, n, [(W2K[su_][:, k, :], asrc(k, seg)[0]) for k in range(8)], [("w2k", su_)] + ards)
                    i = nxt("sg", 2)
                    P.op("act", lambda e, i=i, bg=bg, cg=cg, n=n: e.activation(out=SG[i][:, 0:n], in_=PBX[bg][:, cg:cg + n], func=AF.Silu),
                         reads=[bk(bg)], writes=[("sg", i)])
                    P.op("dve", lambda e, i=i, j=j, bu=bu, cu=cu, n=n, l0=l0: e.tensor_tensor(
                        out=BIG[:, j, l0:l0 + n], in0=SG[i][:, 0:n], in1=PBX[bu][:, cu:cu + n], op=ALU.mult),
                        reads=[("sg", i), bk(bu)], writes=[("B", j)])
            if hoist:
                prenorm(gpre, SEGS[p + 1][0:1], a2=True)
                if not (layer == 2 and which == 0):
                    carry["p0"] = (w2k_load(wg[:, 0:128]), w2k_load(wu[:, 0:128]))
            pend2 = None
            for m in range(8):
                if m + 1 < 8:
                    dslots.append(wd_load(wdn[:, (m + 1) * 128:(m + 2) * 128]))
                sd = dslots[m]
                for seg in segs:
                    n, l0, si = seg["n"], seg["l0"], seg["si"]
                    b = bank()
                    mm_acc(b, 0, n, [(WD[sd][:, j, :], BIG[:, j, l0:l0 + n]) for j in range(11)],
                           [("wd", sd, 0)] + [("B", j) for j in range(11)], fin=False)
                    mm_acc(b, 0, n, [(WD[sd][:, j, :], BIG[:, j, l0:l0 + n]) for j in range(11, NJ)],
                           [("wd", sd, 1)] + [("B", j) for j in range(11, NJ)], first=False)
                    if pend2 is not None:
                        stats_mm(*pend2)
                    sqi = stats_sq(PB[b][:, 0:n], n, [("ps", b)], ser=[("psr", b)])
                    pend2 = (sqi, n, si, m == 0, m == 7)
                    P.op("dve", lambda e, m=m, b=b, n=n, l0=l0: e.tensor_copy(out=Fb[:, m, l0:l0 + n], in_=PB[b][:, 0:n]),
                         reads=[("ps", b), ("psr", b)], writes=kF(m))
            stats_mm(*pend2)
            postnorm(gpre + 1, segs)

        def pool_mixer(layer, p):
            segs = SEGS[p]
            prenorm(layer * 6 + 2, segs)
            Ukeys = [("U", m) for m in range(8)]
            if p == 0:
                P.op("dve", lambda e: e.memset(U[:, :, 0:16], 0.0), reads=Ukeys, writes=Ukeys)
                for c in range(8):
                    P.dma("sp", ("ld_us", c), lambda e, c=c: e.dma_start(
                        out=Us[:, c, :, 0:15], in_=spT[layer, c * 128:(c + 1) * 128, :].rearrange("p (b r) -> p b r", b=4)),
                        writes=[("Us", c)])
            else:
                P.op("dve", lambda e: e.tensor_copy(out=U[:, :, 0:16], in_=U[:, :, 512:528]), reads=Ukeys, writes=Ukeys)
            win = pwin[layer]
            ss = [w2k_load(win[:, 0:128])]
            for m in range(8):
                if m + 1 < 8:
                    ss.append(w2k_load(win[:, (m + 1) * 128:(m + 2) * 128]))
                s = ss[m]
                for seg in segs:
                    n, l0 = seg["n"], seg["l0"]
                    b = bank()
                    mm_acc(b, 0, n, [(W2K[s][:, k, :], Ab[:, k, l0:l0 + n]) for k in range(8)],
                           [("w2k", s)] + [x for k in range(8) for x in kA(k)])
                    if n == 512:
                        P.op("dve", lambda e, m=m, b=b: e.tensor_copy(out=U[:, m, 16:528], in_=PB[b][:, 0:512]),
                             reads=[("ps", b)], writes=[("U", m)])
                    else:
                        P.op("dve", lambda e, m=m, b=b: e.tensor_copy(
                            out=Us[:, m, :, 15:23], in_=PB[b][:, 0:32].rearrange("p (b t) -> p b t", b=4)),
                            reads=[("ps", b)], writes=[("Us", m)])
                if p == 3:
                    b = bank()
                    mm_acc(b, 0, 128, [(Ab[:, k, 497:544], W2K[s][:, k, :]) for k in range(8)],
                           [("w2k", s)] + [x for k in range(8) for x in kA(k)], M=47)
                    P.op("dve", lambda e, m=m, b=b: e.tensor_copy(out=TMP_tok[0:47, m * 128:(m + 1) * 128], in_=PB[b][0:47, 0:128]),
                         reads=[("ps", b)], writes=["TMPtok"])
            if p == 3:
                key = ("o", "pool_p", layer)
                P.dma("sp", ("pool_p", layer), lambda e: e.dma_start(out=pool_p[layer], in_=TMP_tok[0:15, :]), reads=["TMPtok"], writes=[key])
                out_keys.append(key)
                key = ("o", "pool_s", layer)
                P.dma("sp", ("pool_s", layer), lambda e: e.dma_start(out=pool_s[layer, :, 7:15, :], in_=TMP_tok[15:47, :]),
                      reads=["TMPtok"], writes=[key])
                out_keys.append(key)
            for m in range(8):
                gi = m // 2
                w = WINS[gi]
                for seg in segs:
                    if seg["n"] == 512:
                        src = U[:, m, :]
                        bufs = (PSa, PSb)
                        sh = 1
                        cur = src
                        for lv in range(gi + 1):
                            dst = bufs[lv % 2]
                            lo = 2 * sh - 1
                            P.op("dve", lambda e, dst=dst, cur=cur, lo=lo, sh=sh: e.tensor_tensor(
                                out=dst[:, lo:528], in0=cur[:, lo:528], in1=cur[:, lo - sh:528 - sh], op=ALU.add),
                                reads=[("U", m), ("psc", 0), ("psc", 1)], writes=[("psc", lv % 2)])
                            cur = dst
                            sh *= 2
                        P.op("dve", lambda e, m=m, cur=cur, w=w: e.scalar_tensor_tensor(
                            out=BIG[:, m, 0:512], in0=cur[:, 16:528], scalar=1.0 / w, in1=U[:, m, 16:528],
                            op0=ALU.mult, op1=ALU.subtract), reads=[("psc", gi % 2), ("U", m)], writes=[("B", m)])
                        if p == 0:
                            P.op("dve", lambda e, cur=cur, gi=gi: e.tensor_tensor(out=T16, in0=cur[:, 16:32], in1=RCT[:, gi, :], op=ALU.mult),
                                 reads=[("psc", gi % 2), "RCT"], writes=["T16"])
                            P.op("dve", lambda e, m=m: e.tensor_tensor(out=BIG[:, m, 0:16], in0=T16, in1=U[:, m, 16:32], op=ALU.subtract),
                                 reads=["T16", ("U", m)], writes=[("B", m)])
                    else:
                        src = Us[:, m, :, :]
                        bufs = (SSa, SSb)
                        sh = 1
                        cur = src
                        for lv in range(gi + 1):
                            dst = bufs[lv % 2]
                            lo = 2 * sh - 1
                            P.op("dve", lambda e, dst=dst, cur=cur, lo=lo, sh=sh: e.tensor_tensor(
                                out=dst[:, :, lo:23], in0=cur[:, :, lo:23], in1=cur[:, :, lo - sh:23 - sh], op=ALU.add),
                                reads=[("Us", m), ("ssc", 0), ("ssc", 1)], writes=[("ssc", lv % 2)])
                            cur = dst
                            sh *= 2
                        P.op("dve", lambda e, m=m, cur=cur, w=w: e.scalar_tensor_tensor(
                            out=BIG[:, m, 512:544].rearrange("p (b t) -> p b t", b=4), in0=cur[:, :, 15:23], scalar=1.0 / w,
                            in1=Us[:, m, :, 15:23], op0=ALU.mult, op1=ALU.subtract),
                            reads=[("ssc", gi % 2), ("Us", m)], writes=[("B", m)])
            sgw = wd_load(pwgrp[layer], kind="g")
            WG = WD[sgw][:].rearrange("p j m -> p (j m)")[:, 0:2048].rearrange("p (g k n) -> p g k n", g=4, k=2)
            for gi in range(4):
                for mo in range(2):
                    mch = 2 * gi + mo
                    for seg in segs:
                        n, l0 = seg["n"], seg["l0"]
                        b = bank()
                        mm_acc(b, 0, n, [(WG[:, gi, ki, mo * 128:(mo + 1) * 128], BIG[:, 2 * gi + ki, l0:l0 + n]) for ki in range(2)],
                               [("wd", sgw, 0), ("wd", sgw, 1), ("B", 2 * gi), ("B", 2 * gi + 1)])
                        P.op("dve", lambda e, mch=mch, b=b, n=n, l0=l0: e.tensor_scalar(
                            out=BIG[:, 8 + mch, l0:l0 + n], in0=PB[b][:, 0:n], scalar1=GV[:, mch, 25 + layer:26 + layer], scalar2=None,
                            op0=ALU.mult), reads=[("ps", b), "GV"], writes=[("B", 8 + mch)])
            wout = pwout[layer]
            out_proj(lambda m: wout[:, m * 128:(m + 1) * 128], [8 + k for k in range(8)], segs)
            postnorm(layer * 6 + 3, segs)

        def kv_phase(p):
            segs = SEGS[p]
            prenorm(24, segs)
            h0 = 512 * p
            areads = [x for k in range(8) for x in kA(k)]
            for which, wsrc, dst in (("k", wk_d, k_p), ("v", wv_d, v_p)):
                ss = [w2k_load(wsrc[:, 0:128])]
                for m in range(8):
                    if m + 1 < 8:
                        ss.append(w2k_load(wsrc[:, (m + 1) * 128:(m + 2) * 128]))
                    s = ss[m]
                    if which == "k":
                        b = bank()
                        mm_acc(b, 0, 512, [(W2K[s][:, k, :], Ab[:, k, 0:512]) for k in range(8)], [("w2k", s)] + areads)
                        P.op("act", lambda e, m=m, b=b, h0=h0: e.activation(out=KT[:, m, h0:h0 + 512], in_=PB[b][:, 0:512], func=AF.Copy),
                             reads=[("ps", b)], writes=[("KT", m, p)])
                        if p == 3:
                            b = bank()
                            mm_acc(b, 0, 32, [(W2K[s][:, k, :], Ab[:, k, 512:544]) for k in range(8)], [("w2k", s)] + areads)
                            P.op("act", lambda e, m=m, b=b: e.activation(out=KTs[:, m, :], in_=PB[b][:, 0:32], func=AF.Copy),
                                 reads=[("ps", b)], writes=["KTs"])
                    b = bank()
                    for tt in range(4):
                        mm_acc(b, tt * 128, 128, [(Ab[:, k, tt * 128:(tt + 1) * 128], W2K[s][:, k, :]) for k in range(8)],
                               [("w2k", s)] + areads)
                    i = nxt("ks", 2)
                    P.op("dve", lambda e, i=i, b=b: e.tensor_copy(out=KS[i][:].rearrange("p t x -> p (t x)"), in_=PB[b][:, 0:512]),
                         reads=[("ps", b)], writes=[("ks", i)])
                    key = ("o", which + "p", p, m)
                    P.dma("sp", ("ks", i), lambda e, i=i, m=m, dst=dst, h0=h0: e.dma_start(
                        out=dst[h0:h0 + 512, m * 128:(m + 1) * 128].rearrange("(t p) x -> p t x", p=128), in_=KS[i][:]),
                        reads=[("ks", i)], writes=[key])
                    out_keys.append(key)
                    if p == 3:
                        b = bank()
                        mm_acc(b, 0, 128, [(Ab[:, k, 512:544], W2K[s][:, k, :]) for k in range(8)], [("w2k", s)] + areads, M=32)
                        i = nxt("ks", 2)
                        P.op("dve", lambda e, i=i, b=b: e.tensor_copy(out=KS[i][0:32, 0, :], in_=PB[b][0:32, 0:128]),
                             reads=[("ps", b)], writes=[("ks", i)])
                        dsts = k_s if which == "k" else v_s
                        key = ("o", which + "s_new", m)
                        P.dma("sp", ("ks", i), lambda e, i=i, m=m, dsts=dsts: e.dma_start(
                            out=dsts[:, S - 8:S, m * 128:(m + 1) * 128], in_=KS[i][0:32, 0, :]), reads=[("ks", i)], writes=[key])
                        out_keys.append(key)
                        if which == "v":
                            P.op("act", lambda e, m=m, b=b: e.activation(
                                out=VsN[0:32, 256 * m:256 * m + 256].rearrange("p (j x) -> p j x", x=64)[:, 0:4:3, :],
                                in_=PB[b][0:32, 0:128].rearrange("p (j x) -> p j x", x=64), func=AF.Copy), reads=[("ps", b), ("ks", i)], writes=["VsN"])

        def vaug(vt, h, nk):
            hl = h % 8
            return vt[0:nk, 128 * hl:128 * hl + 128]

        def vs_load(src_rows, nk, reads, half):
            s = nxt("vs", NVS)
            for j in range(2):
                P.dma("pool", ("vs", s, j), lambda e, s=s, src_rows=src_rows, nk=nk, j=j, half=half: e.dma_start(
                    out=VS[s][0:nk, :].rearrange("p (c x) -> p c x", x=256)[:, :, 192 * j:192 * j + 64],
                    in_=src_rows[:, 512 * half:512 * half + 512].rearrange("p (c x) -> p c x", x=128)[:, :, 64 * j:64 * j + 64]),
                    reads=reads, writes=[("vsj", s, j)])
            return s

        def sl(start, cnt, step):
            return slice(start, start + step * (cnt - 1) + 1, step)

        def attn_units(p):
            h0 = 512 * p
            units = []
            for qb in range(4):
                T = h0 + 128 * qb
                blocks = []
                if T > 0:
                    blocks.append((sl(T - 128, 128, 1), sl(T - 128, 128, 1), 128, {(T - 128) // 512}))
                blocks.append((sl(T, 128, 1), sl(T, 128, 1), 128, {p}))
                units.append((0, 128, sl(128 * qb, 128, 1), blocks, (0 if T > 0 else 128)))
            for r in range(4):
                blocks = []
                if p > 0:
                    blocks.append((sl(h0 - 512 + r, 128, 4), sl(h0 - 512 + r, 128, 4), 128, {p - 1}))
                blocks.append((sl(h0 + r, 128, 4), sl(h0 + r, 128, 4), 128, {p}))
                units.append((1, 128, sl(r, 128, 4), blocks, (0 if p > 0 else 128)))
            nk = 32 * (p + 1)
            for r in range(16):
                blocks = [(sl(r, nk, 16), sl(r, nk, 16), nk, set(range(p + 1)))]
                units.append((2, 32, sl(r, 32, 16), blocks, 256 + 32 * p))
            return units

        def attn_mixer(layer, p):
            segs = SEGS[p]
            jl = layer - 2
            prenorm(layer * 6 + 2, segs)
            areads = [x for k in range(8) for x in kA(k)]
            wq = wq_d[jl]
            cols = [(c, g) for c in range(8) for g in range(3)]
            ss = [w2k_load(wq[:, cols[0][1] * D + cols[0][0] * 128:cols[0][1] * D + cols[0][0] * 128 + 128])]
            for i, (c, g) in enumerate(cols):
                if i + 1 < len(cols):
                    c2, g2 = cols[i + 1]
                    ss.append(w2k_load(wq[:, g2 * D + c2 * 128:g2 * D + c2 * 128 + 128]))
                s = ss[i]
                for seg in segs:
                    n, l0 = seg["n"], seg["l0"]
                    b = bank()
                    mm_acc(b, 0, n, [(W2K[s][:, k, :], Ab[:, k, l0:l0 + n]) for k in range(8)], [("w2k", s)] + areads)
                    P.op("act", lambda e, c=c, g=g, b=b, n=n, l0=l0: e.activation(out=BIG[:, 3 * c + g, l0:l0 + n], in_=PB[b][:, 0:n], func=AF.Copy),
                         reads=[("ps", b)], writes=[("B", 3 * c + g)])
            units = attn_units(p)
            LAG = 3
            for hf in range(2):
                items = [(ui, hl) for ui in range(len(units)) for hl in range(8)]
                uvs = {}

                def stage_a(ui, hl):
                    (g, nq, qsl, blocks, nscol) = units[ui]
                    if ui not in uvs:
                        uvs[ui] = [vs_load(v_p[vrows, :], nk, [("o", "vp", pp, m) for pp in kps for m in range(8)], hf)
                                   for (ksl, vrows, nk, kps) in blocks]
                    nb = len(blocks)
                    h = 8 * hf + hl
                    c, r0 = h // 2, 64 * (h % 2)
                    bs = bank()
                    for bi, (ksl, vrows, nk, kps) in enumerate(blocks):
                        mm_acc(bs, bi * nq, nq, [(KT[r0:r0 + 64, c, ksl], BIG[r0:r0 + 64, 3 * c + g, qsl])],
                               [("KT", c, pp) for pp in kps] + [("B", 3 * c + g)], M=nk)
                    nkm = blocks[0][2]
                    i = nxt("tt", 4)
                    cf = -slope(h) * DILS[g] / SCALE
                    P.op("dve", lambda e, i=i, bs=bs, nkm=nkm, w=nb * nq, nscol=nscol, cf=cf: e.scalar_tensor_tensor(
                        out=TT[i][0:nkm, 0:w], in0=NST[0:nkm, nscol:nscol + w], scalar=cf, in1=PB[bs][0:nkm, 0:w],
                        op0=ALU.mult, op1=ALU.add), reads=[("ps", bs), "NST"], writes=[("tt", i)])
                    P.op("act", lambda e, i=i, nkm=nkm, w=nb * nq: e.activation(out=PT[i][0:nkm, 0:w], in_=TT[i][0:nkm, 0:w], func=AF.Exp,
                                                                              scale=SCALE), reads=[("tt", i)], writes=[("pt", i)])
                    return i

                def stage_b(ui, hl, i):
                    (g, nq, qsl, blocks, nscol) = units[ui]
                    vsl = uvs[ui]
                    nb = len(blocks)
                    h = 8 * hf + hl
                    bo = bank()
                    mm_acc(bo, 0, nq, [(vaug(VS[vsl[bi]], h, blocks[bi][2]), PT[i][0:blocks[bi][2], bi * nq:(bi + 1) * nq]) for bi in range(nb)],
                           [("pt", i)] + [("vsj", s_, j_) for s_ in vsl for j_ in range(2)])
                    if g == 0:
                        P.op("dve", lambda e, hl=hl, bo=bo, qsl=qsl, nq=nq: e.tensor_copy(out=Fb[:, hl, qsl], in_=PB[bo][:, 0:nq]),
                             reads=[("ps", bo)], writes=kF(hl))
                    else:
                        P.op("dve", lambda e, hl=hl, bo=bo, qsl=qsl, nq=nq: e.tensor_tensor(
                            out=Fb[:, hl, qsl], in0=Fb[:, hl, qsl], in1=PB[bo][:, 0:nq], op=ALU.add),
                            reads=[("ps", bo)] + kF(hl), writes=kF(hl))

                st_i = {}
                for idx in range(len(items) + LAG):
                    if idx < len(items):
                        st_i[idx] = stage_a(*items[idx])
                    if idx - LAG >= 0:
                        stage_b(*items[idx - LAG], st_i.pop(idx - LAG))
                for hl in range(8):
                    h = 8 * hf + hl
                    c, r0 = h // 2, 64 * (h % 2)
                    b = bank()
                    P.op("pe", lambda e, b=b, hl=hl: e.matmul(PB[b][:, 0:512], lhsT=PSW[:], rhs=Fb[:, hl, 0:512], start=True, stop=True),
                         reads=kF(hl) + ["PSW"], writes=[("ps", b)])
                    P.op("act", lambda e, b=b, r0=r0: e.activation(out=LNB[r0:r0 + 64, :], in_=PB[b][r0:r0 + 64, 0:512], func=AF.Ln),
                         reads=[("ps", b)], writes=[("tmps", 0)])
                    P.op("act", lambda e, r0=r0: e.activation(out=REC[r0:r0 + 64, :], in_=LNB[r0:r0 + 64, :], func=AF.Exp, scale=-1.0),
                         reads=[("tmps", 0)], writes=[("tmps", 1)])
                    P.op("dve", lambda e, c=c, hl=hl, r0=r0: e.tensor_tensor(
                        out=BIG[r0:r0 + 64, 3 * c, 0:512], in0=Fb[r0:r0 + 64, hl, 0:512], in1=REC[r0:r0 + 64, :], op=ALU.mult),
                        reads=kF(hl) + [("tmps", 1)], writes=[("B", 3 * c)])
            if p == 3 and not DBG.get("nosample"):
                sample_attn()
            wo = wo_d[jl]
            out_proj(lambda m: wo[:, m * 128:(m + 1) * 128], [3 * k for k in range(8)], segs)
            postnorm(layer * 6 + 3, segs)

        def sample_attn():
            tiles = [(0, "c", 1920, 0)] + [(1, "c", 1536 + 128 * i, 1 + i) for i in range(4)] + [(2, "s", r, 5 + r) for r in range(8)]
            sa = DBG.get("sa", {})
            QZ = Fb[:].rearrange("p c n -> p (c n)")[:, 0:768].bitcast(BF16).rearrange("p (g h t) -> p g h t", g=3, h=16)
            qzk = kF(0) + kF(1)
            P.op("pool", lambda e: e.memset(QZ, 0.0), writes=qzk)
            for g in range(3):
                for hh in range(2):
                    r0 = 64 * hh
                    P.op("dve", lambda e, g=g, hh=hh, r0=r0: e.tensor_copy(out=QZ[r0:r0 + 64, g, hh:16:2, :], in_=BIG[r0:r0 + 64, g:24:3, 512:544]),
                         reads=[("B", 3 * c + g) for c in range(8)] + qzk, writes=qzk)
            for b in range(sa.get("nb", 4)):
                first = [True]
                qc = 512 + 8 * b

                def do_tile(g, nk, kT_fn, kreads, vt, vreads, tab_src):
                    i = nxt("sbi", 1)
                    P.dma("sp", ("sbi", i), lambda e, i=i, tab_src=tab_src, nk=nk: e.dma_start(out=SBI[i][0:nk, :], in_=tab_src),
                          writes=[("sbi", i)])
                    bs = bank()

                    def fs(e, bs=bs, nk=nk, g=g, kT_fn=kT_fn, b=b):
                        for h in range(16):
                            c, r0 = h // 2, 64 * (h % 2)
                            ins = e.matmul(PB[bs][0:nk, h * 8:(h + 1) * 8], lhsT=kT_fn(c, r0), rhs=QZ[:, g, h, 8 * b:8 * b + 8],
                                           start=(h == 0), stop=(h == 15), skip_group_check=True)
                        return ins
                    steps = sa.get("steps", 31)
                    if steps & 2:
                        P.op("pe", fs, reads=kreads + qzk, writes=[("ps", bs)])
                    ti = nxt("tt", 4)
                    if steps & 4:
                      P.op("dve", lambda e, ti=ti, bs=bs, nk=nk, i=i: e.scalar_tensor_tensor(
                        out=TT[ti][0:nk, 0:128], in0=PB[bs][0:nk, 0:128], scalar=SCALE, in1=SBI[i][0:nk, :], op0=ALU.mult, op1=ALU.add),
                        reads=[("ps", bs), ("sbi", i)], writes=[("tt", ti)])
                    if steps & 8:
                      P.op("act", lambda e, ti=ti, nk=nk: e.activation(out=PT[ti][0:nk, 0:128], in_=TT[ti][0:nk, 0:128], func=AF.Exp),
                         reads=[("tt", ti)], writes=[("pt", ti)])
                    fl = first[0]
                    first[0] = False

                    def fo(e, nk=nk, vt=vt, ti=ti, fl=fl):
                        for h in range(16):
                            ins = e.matmul(PST[1][:, h * 8:(h + 1) * 8], lhsT=vaug(vt[h // 8], h, nk), rhs=PT[ti][0:nk, h * 8:(h + 1) * 8],
                                           start=(fl and h == 0), stop=False, skip_group_check=True)
                        return ins
                    if steps & 16:
                        P.op("pe", fo, reads=[("pt", ti)] + vreads, writes=[("pst", 1)])

                for (g, kind, r, tab) in (tiles if sa.get("cache", True) else []):
                    if kind not in sa.get("kinds", "cs"):
                        continue
                    rows = sl(r, 128, 1) if kind == "c" else sl(r, 128, 16)
                    ki = nxt("kr", 2)
                    P.dma("pool", ("kr", ki), lambda e, ki=ki, rows=rows, b=b: e.dma_start(out=KR[ki], in_=ck[b, rows, :]), writes=[("sg", ki)])

                    def ftr(e, ki=ki):
                        for c in range(8):
                            ins = e.transpose(PTR[:, c * 128:(c + 1) * 128], KR[ki][:, c * 128:(c + 1) * 128], IDN[:])
                        return ins
                    P.op("pe", ftr, reads=[("sg", ki), "IDN"], writes=["ptr"])
                    P.op("act", lambda e, ki=ki: e.activation(out=KsT[ki].rearrange("p c n -> p (c n)"), in_=PTR[:, :], func=AF.Copy),
                         reads=["ptr"], writes=[("ks", ki)])
                    vsa = vs_load(cv[b, rows, :], 128, [], 0)
                    vsb = vs_load(cv[b, rows, :], 128, [], 1)
                    do_tile(g, 128, lambda c, r0, ki=ki: KsT[ki][:, c, :], [("ks", ki)], (VS[vsa], VS[vsb]),
                            [("vsj", v_, j_) for v_ in (vsa, vsb) for j_ in range(2)], sbias_d[tab])
                for g in range(3 if sa.get("new", True) else 0):
                    do_tile(g, 32, lambda c, r0: KTs[:, c, :], ["KTs"], (VsN[:, 0:1024], VsN[:, 1024:2048]), ["VsN"], sbias_d[13 + 4 * g + b, 0:32, :])
                if not sa.get("norm", True):
                    continue
                P.op("dve", lambda e: e.tensor_copy(out=S1[:], in_=PST[1][:, 0:128]), reads=[("pst", 1)], writes=["S1"])
                b2 = bank()
                P.op("pe", lambda e, b2=b2: e.matmul(PB[b2][:, 0:128], lhsT=PSW[:], rhs=S1[:], start=True, stop=True),
                     reads=["S1", "PSW"], writes=[("ps", b2)])
                for hh in range(2):
                    r0 = 64 * hh
                    pv = PB[b2][r0:r0 + 64, 0:128].rearrange("p (c x t) -> p c x t", c=8, x=2)[:, :, hh, :]
                    lv = LNB[r0:r0 + 64, 0:64].rearrange("p (c t) -> p c t", c=8)
                    rv = REC[r0:r0 + 64, 0:64].rearrange("p (c t) -> p c t", c=8)
                    sv = S1[r0:r0 + 64, :].rearrange("p (c x t) -> p c x t", c=8, x=2)[:, :, hh, :]
                    P.op("act", lambda e, pv=pv, lv=lv: e.activation(out=lv, in_=pv, func=AF.Ln), reads=[("ps", b2)], writes=[("tmps", 0)])
                    P.op("act", lambda e, lv=lv, rv=rv: e.activation(out=rv, in_=lv, func=AF.Exp, scale=-1.0), reads=[("tmps", 0)], writes=[("tmps", 1)])
                    P.op("dve", lambda e, sv=sv, rv=rv, r0=r0, qc=qc: e.tensor_tensor(
                        out=BIG[r0:r0 + 64, 0:24:3, qc:qc + 8], in0=sv, in1=rv, op=ALU.mult),
                        reads=["S1", ("tmps", 1)], writes=[("B", 3 * c) for c in range(8)])

        def body():
            for layer in range(4):
                if layer == 2:
                    P.barrier()
                for p in range(4):
                    if layer == 2:
                        kv_phase(p)
                    ffn(layer, 0, p, pre=(p == 0), hoist=(p < 3))
                    if stop == ("ffn0", layer, p):
                        return
                for p in range(4):
                    if layer < 2:
                        pool_mixer(layer, p)
                    else:
                        attn_mixer(layer, p)
                    if stop == ("mix", layer, p):
                        return
                for p in range(4):
                    ffn(layer, 1, p, pre=(p == 0), hoist=(p < 3))
                    if stop == ("ffn1", layer, p):
                        return
        body()
        key = ("o", "yT")
        P.dma("sp", "st_y", lambda e: e.dma_start(out=yT.rearrange("(c p) n -> p c n", p=128), in_=H[:]),
              reads=[("h", c, sid) for c in range(8) for sid in (0, 1, 2, 3, "s")], writes=[key])
        out_keys.append(key)
        P.op("sp", None, reads=out_keys)
        P.emit()
    return nc


def host_tables():
    ident = np.eye(128, dtype=np.float32)
    psw = np.zeros((128, 128), np.float32)
    for i in range(128):
        psw[i, (i + 64) % 128] = 1.0
    k = np.arange(128)[:, None]
    q = np.arange(128)[None, :]
    ns_prev = np.where(k >= q, q + 128 - k, BIGN).astype(np.float32)
    ns_cur = np.where(q >= k, q - k, BIGN).astype(np.float32)
    ns2 = np.zeros((128, 128), np.float32)
    for p in range(4):
        j = np.arange(32)[None, :]
        st = 32 * p + j - k
        ns2[:, 32 * p:32 * p + 32] = np.where(st >= 0, st, BIGN)
    nstab = np.concatenate([ns_prev, ns_cur, ns2], axis=1).astype(np.float32)
    rct = np.zeros((128, 4, 16), np.float32)
    for gi, w in enumerate(WINS):
        rct[:, gi, :] = 1.0 / np.minimum(np.arange(16) + 1, w)
    rctab = rct.reshape(128, 64)
    sl_h = np.array([slope(h) for h in range(16)], np.float64)
    t = np.arange(8)
    sbias = np.full((25, 128, 128), NEGB, np.float64)

    def fill(tab, rows, g, valid_extra=None):
        dil = DILS[g]
        dist = 2048 + t[None, :] - rows[:, None]
        ok = (dist >= 0) & (dist % dil == 0) & (dist // dil <= 128)
        if valid_extra is not None:
            ok = ok & valid_extra
        for h in range(16):
            vals = np.where(ok, -sl_h[h] * dist, NEGB)
            sbias[tab, :rows.shape[0], h * 8:(h + 1) * 8] = vals
    fill(0, 1920 + np.arange(128), 0)
    for i in range(4):
        fill(1 + i, 1536 + 128 * i + np.arange(128), 1)
    for r in range(8):
        fill(5 + r, r + 16 * np.arange(128), 2)
    for g in range(3):
        for b in range(4):
            pidx = np.arange(32)
            rows = 2048 + (pidx % 8)
            ok = ((pidx // 8) == b)[:, None] & np.ones((1, 8), bool)
            fill(13 + 4 * g + b, rows, g, ok)
    return ident, psw, nstab, rctab, sbias.astype(np.float32)


_CACHE = {}


def kernel(x_prompt, x_sample, state_pool, cache_k, cache_v, norm_g, ffn_w_gate, ffn_w_up, ffn_w_down,
           pool_w_in, pool_w_grp, pool_scale, pool_w_out, kv_norm, w_k, w_v, attn_w_q, attn_w_o):
    f32 = np.float32
    a = lambda v: np.ascontiguousarray(np.asarray(v, dtype=f32))
    x_prompt, x_sample, state_pool, cache_k, cache_v = map(a, (x_prompt, x_sample, state_pool, cache_k, cache_v))
    if "nc" not in _CACHE:
        _CACHE["nc"] = build_program()
    nc = _CACHE["nc"]
    ident, psw, nstab, rctab, sbias = host_tables()
    vecs = np.concatenate([a(norm_g).reshape(24, D), a(kv_norm).reshape(1, D), a(pool_scale).reshape(2, D)], axis=0)
    gvec = np.ascontiguousarray(vecs.reshape(NVEC, 8, 128).transpose(2, 1, 0)).reshape(128, 8 * NVEC)
    shared = {
        "gvec": gvec, "wgate": a(ffn_w_gate), "wup": a(ffn_w_up), "wdown": a(ffn_w_down), "pwin": a(pool_w_in),
        "pwgrp": a(pool_w_grp), "pwout": a(pool_w_out), "wk": a(w_k), "wv": a(w_v), "wq": a(attn_w_q), "wo": a(attn_w_o),
        "ident": ident, "psw": psw, "nstab": nstab, "rctab": rctab, "sbias": sbias,
    }
    in_maps = []
    for i in range(8):
        xs = x_sample[4 * i:4 * i + 4].reshape(NSM, D)
        xT = np.ascontiguousarray(np.concatenate([x_prompt[i], xs], axis=0).T)
        sp = state_pool[:, 4 * i:4 * i + 4]
        spT = np.ascontiguousarray(sp.transpose(0, 3, 1, 2)).reshape(2, D, 60)
        sprow = np.ascontiguousarray(sp[:, :, 8:15, :])
        m = dict(shared)
        m.update({"xT": xT, "spT": spT, "sprow": sprow,
                  "ck": np.ascontiguousarray(cache_k[4 * i:4 * i + 4].reshape(4, S, D)),
                  "cv": np.ascontiguousarray(cache_v[4 * i:4 * i + 4].reshape(4, S, D))})
        in_maps.append(m)
    res = run_bass_kernel_spmd(nc, in_maps, core_ids=list(range(8)))
    R = res.results
    y_prompt = np.stack([R[i]["yT"][:, :S].T for i in range(8)], axis=0)
    y_sample = np.concatenate([R[i]["yT"][:, S:].T.reshape(4, 8, D) for i in range(8)], axis=0)
    pool_p = np.stack([R[i]["pool_p"] for i in range(8)], axis=1)
    k_p = np.stack([R[i]["k_p"].reshape(S, 16, 64) for i in range(8)], axis=0)
    v_p = np.stack([R[i]["v_p"].reshape(S, 16, 64) for i in range(8)], axis=0)
    pool_s = np.concatenate([R[i]["pool_s"] for i in range(8)], axis=1)
    k_s = np.concatenate([R[i]["k_s"].reshape(4, S, 16, 64) for i in range(8)], axis=0)
    v_s = np.concatenate([R[i]["v_s"].reshape(4, S, 16, 64) for i in range(8)], axis=0)
    c = lambda v: np.ascontiguousarray(v, dtype=f32)
    return (c(y_prompt), c(y_sample), c(pool_p), c(k_p), c(v_p), c(pool_s), c(k_s), c(v_s))
```

```python
import contextlib
import numpy as np
import concourse.bass as bass
import concourse.mybir as mybir
from concourse.bass_utils import run_bass_kernel_spmd

F32 = mybir.dt.float32
BF16 = mybir.dt.bfloat16
AF = mybir.ActivationFunctionType
ALU = mybir.AluOpType

D = 1024
NJ = 22
S = 2048
NSM = 32
HT = S + NSM
NTOK = 544
SCALE = 0.125
EPS = 1e-6
BIGN = 1.0e5
NEGB = -30000.0
WINS = (2, 4, 8, 16)
DILS = (1, 4, 16)
NVEC = 27


class Prog:
    ENG = ("pe", "act", "dve", "pool", "sp")

    def __init__(self, nc):
        self.nc = nc
        self.ops = []
        self.last_w = {}
        self.readers = {}

    def op(self, eng, fn, reads=(), writes=(), lane=None, extra=()):
        deps = set(extra)
        for k in reads:
            w = self.last_w.get(k)
            if w is not None:
                deps.add(w)
        for k in writes:
            w = self.last_w.get(k)
            if w is not None:
                deps.add(w)
            rd = self.readers.get(k)
            if rd:
                deps.update(rd.values())
        idx = len(self.ops)
        self.ops.append([eng, fn, deps, lane, False, 0])
        rkey = (eng, idx) if lane is not None else eng
        for k in reads:
            self.readers.setdefault(k, {})[rkey] = idx
        for k in writes:
            self.last_w[k] = idx
            self.readers[k] = {}
        return idx

    def dma(self, queue, lane, fn, reads=(), writes=()):
        return self.op(queue, fn, reads, writes, lane=lane)

    def barrier(self):
        last = {}
        for i, o in enumerate(self.ops):
            if o[1] is None:
                continue
            if o[3] is not None:
                last[("l", o[3])] = i
            else:
                last[("e", o[0])] = i
        deps = set(last.values())
        for e in self.ENG:
            self.op(e, None, extra=deps)

    def emit(self):
        nc = self.nc
        ops = self.ops
        for o in ops:
            for d in o[2]:
                dd = ops[d]
                if dd[3] is None and dd[0] == "pe" and o[0] == "pe" and o[3] is None:
                    continue
                dd[4] = True
        cnt = {e: 0 for e in self.ENG}
        lane_cnt = {}
        for o in ops:
            if o[3] is not None:
                lane_cnt[o[3]] = lane_cnt.get(o[3], 0) + 16
                o[5] = lane_cnt[o[3]]
                o[4] = True
            elif o[4]:
                cnt[o[0]] += 1
                o[5] = cnt[o[0]]
        lane_names = sorted(lane_cnt.keys(), key=str)
        with contextlib.ExitStack() as st:
            sems = {e: st.enter_context(nc.semaphore("s_" + e)) for e in self.ENG if e != "sp"}
            lsem = {}
            for i, ln in enumerate(lane_names):
                lsem[ln] = st.enter_context(nc.semaphore("l%d" % i))
            block = st.enter_context(nc.Block())
            per_eng = {e: [] for e in self.ENG}
            for i, o in enumerate(ops):
                per_eng[o[0]].append(i)

            def run(engname, engobj):
                waited = {}
                for i in per_eng[engname]:
                    eng, fn, deps, lane, sig, val = ops[i]
                    need = {}
                    for d in deps:
                        dd = ops[d]
                        if dd[3] is not None:
                            key = ("l", dd[3])
                        else:
                            if dd[0] == "pe" and engname == "pe" and lane is None:
                                continue
                            key = ("e", dd[0])
                        if dd[5] > need.get(key, 0):
                            need[key] = dd[5]
                    for key, v in need.items():
                        if waited.get(key, 0) >= v:
                            continue
                        waited[key] = v
                        s = lsem[key[1]] if key[0] == "l" else sems[key[1]]
                        engobj.wait_ge(s, v)
                    if fn is None:
                        continue
                    ins = fn(engobj)
                    if lane is not None:
                        ins.then_inc(lsem[lane], 16)
                    elif sig:
                        ins.then_inc(sems[engname], 1)

            @block.tensor
            def _(e):
                run("pe", e)

            @block.scalar
            def _(e):
                run("act", e)

            @block.vector
            def _(e):
                run("dve", e)

            @block.gpsimd
            def _(e):
                run("pool", e)

            @block.sync
            def _(e):
                run("sp", e)


DBG = {}


def slope(h):
    return 2.0 ** (-(h + 1) / 2.0)


def build_program(stop=None):
    nc = bass.Bass("TRN2", target_bir_lowering=False)

    def din(name, shape):
        return nc.dram_tensor(name, list(shape), F32, kind="ExternalInput").ap()

    def dout(name, shape):
        return nc.dram_tensor(name, list(shape), F32, kind="ExternalOutput").ap()

    xT = din("xT", [D, HT])
    spT = din("spT", [2, D, 60])
    sprow = din("sprow", [2, 4, 7, D])
    ck = din("ck", [4, S, D])
    cv = din("cv", [4, S, D])
    gvec_d = din("gvec", [128, 8 * NVEC])
    wgate = din("wgate", [4, 2, D, 2816])
    wup = din("wup", [4, 2, D, 2816])
    wdown = din("wdown", [4, 2, 2816, D])
    pwin = din("pwin", [2, D, D])
    pwgrp = din("pwgrp", [2, 4, 256, 256])
    pwout = din("pwout", [2, D, D])
    wk_d = din("wk", [D, D])
    wv_d = din("wv", [D, D])
    wq_d = din("wq", [2, D, 3 * D])
    wo_d = din("wo", [2, D, D])
    ident_d = din("ident", [128, 128])
    psw_d = din("psw", [128, 128])
    nstab_d = din("nstab", [128, 384])
    rctab_d = din("rctab", [128, 64])
    sbias_d = din("sbias", [25, 128, 128])

    yT = dout("yT", [D, HT])
    pool_p = dout("pool_p", [2, 15, D])
    k_p = dout("k_p", [S, D])
    v_p = dout("v_p", [S, D])
    pool_s = dout("pool_s", [2, 4, 15, D])
    k_s = dout("k_s", [4, S, D])
    v_s = dout("v_s", [4, S, D])

    st = contextlib.ExitStack()
    with st:
        def sb(name, shape, dt):
            return st.enter_context(nc.sbuf_tensor(name, list(shape), dt))

        def psum(name, shape, dt):
            return st.enter_context(nc.psum_tensor(name, list(shape), dt))

        H = sb("H", [128, 8, HT], F32)
        Fb = sb("Fb", [128, 8, NTOK], F32)
        Ab = Fb[:].rearrange("p c n -> p (c n)")[:, 0:4 * NTOK].bitcast(BF16).rearrange("p (c n) -> p c n", c=8)
        BIG = sb("BIG", [128, 24, NTOK], BF16)
        KTr = sb("KTr", [128, 8 * S], BF16)
        KT = KTr[:].rearrange("p (c n) -> p c n", c=8)
        KF = KTr[:].bitcast(F32)
        U = KF[:, 0:8 * 528].rearrange("p (c n) -> p c n", c=8)
        Us = KF[:, 4224:4224 + 8 * 92].rearrange("p (c b n) -> p c b n", c=8, b=4)
        PSa = KF[:, 4960:4960 + 528]
        PSb = KF[:, 5488:5488 + 528]
        SSa = KF[:, 6016:6016 + 92].rearrange("p (b n) -> p b n", b=4)
        SSb = KF[:, 6108:6108 + 92].rearrange("p (b n) -> p b n", b=4)
        T16 = KF[:, 6200:6216]
        TMP_tok = KF[:, 6216:6216 + 1024]
        NW2K = 5
        W2K = [sb("w2k%d" % i, [128, 8, 128], BF16) for i in range(NW2K)]
        WD = [sb("wd%d" % i, [128, NJ, 128], BF16) for i in range(2)]
        NVS = 4
        VS = [sb("vs%d" % i, [128, 1024], BF16) for i in range(NVS)]
        VsN = sb("VsN", [32, 2048], BF16)
        KTs = sb("KTs", [128, 8, 32], BF16)
        GV = sb("GV", [128, 8, NVEC], F32)
        ONES = sb("ONES", [128, 128], BF16)
        IDN = sb("IDN", [128, 128], BF16)
        PSW = sb("PSW", [128, 128], F32)
        NST = sb("NST", [128, 384], F32)
        RCT = sb("RCT", [128, 4, 16], F32)
        EPSV = sb("EPSV", [128, 1], F32)
        SQ = [sb("sq%d" % i, [128, 512], BF16) for i in range(2)]
        TMPS = [sb("tmps%d" % i, [128, 512], F32) for i in range(2)]
        RSTD = [sb("rstd0", [128, 512], F32), sb("rstd1", [128, NSM], F32)]
        A2 = sb("A2", [128, 8, 512], BF16)
        SG = [sb("sg%d" % i, [128, 512], F32) for i in range(2)]
        TT = [sb("tt%d" % i, [128, 256], F32) for i in range(4)]
        PT = [sb("pt%d" % i, [128, 256], BF16) for i in range(4)]
        KS = [sb("ks%d" % i, [128, 4, 128], F32) for i in range(2)]
        KR = [SG[i][:].bitcast(BF16) for i in range(2)]
        KsT = [KS[i][:].rearrange("p t x -> p (t x)").bitcast(BF16).rearrange("p (c n) -> p c n", c=8) for i in range(2)]
        SBI = [sb("sbi%d" % i, [128, 128], F32) for i in range(1)]
        S1 = sb("S1", [128, 128], F32)
        LNB = TMPS[0]
        REC = TMPS[1]

        NB = 5
        PB = [psum("pb%d" % i, [128, 512], F32) for i in range(NB)]
        PST = [psum("pst%d" % i, [128, 512], F32) for i in range(2)]
        PTR = psum("ptr", [128, D], BF16)
        PBX = PB + [PST[1]]

        def bk(b):
            return ("ps", b) if b < NB else ("pst", 1)

        P = Prog(nc)
        ctr = {"bank": 0, "w2k": 0, "wd": 0, "vs": 0, "sq": 0, "sg": 0, "tt": 0, "ks": 0, "kr": 0, "sbi": 0}

        def nxt(name, n):
            v = ctr[name]
            ctr[name] = (v + 1) % n
            return v

        def bank():
            return nxt("bank", NB)

        def kA(k):
            return [("A", k), ("F", k // 2)]

        def kF(m):
            return [("F", m)] + ([("A", 2 * m), ("A", 2 * m + 1)] if m < 4 else [])

        SEGS = []
        for p in range(4):
            sg = [dict(h0=512 * p, n=512, l0=0, id=p, si=0)]
            if p == 3:
                sg.append(dict(h0=S, n=NSM, l0=512, id="s", si=1))
            SEGS.append(sg)

        P.dma("sp", "ld_h", lambda e: e.dma_start(out=H[:], in_=xT.rearrange("(c p) n -> p c n", p=128)),
              writes=[("h", c, sid) for c in range(8) for sid in (0, 1, 2, 3, "s")])
        P.dma("sp", "ld_gv", lambda e: e.dma_start(out=GV[:].rearrange("p c v -> p (c v)"), in_=gvec_d), writes=["GV"])
        P.dma("sp", "ld_psw", lambda e: e.dma_start(out=PSW[:], in_=psw_d), writes=["PSW"])
        P.dma("sp", "ld_nst", lambda e: e.dma_start(out=NST[:], in_=nstab_d), writes=["NST"])
        P.dma("sp", "ld_rct", lambda e: e.dma_start(out=RCT[:].rearrange("p g n -> p (g n)"), in_=rctab_d), writes=["RCT"])
        P.dma("pool", "ld_idn", lambda e: e.dma_start(out=IDN[:], in_=ident_d), writes=["IDN"])
        out_keys = []
        for b in range(0 if not DBG.get("nocc") else 0, 4 if not DBG.get("nocc") else 0):
            for nm, src, dst in (("k", ck, k_s), ("v", cv, v_s)):
                key = ("o", "cc", nm, b)
                P.dma("sp", ("cc", nm, b), lambda e, src=src, dst=dst, b=b: e.dma_start(
                    out=dst[b, 0:S - 8, :].rearrange("r d -> (r d)").rearrange("(a x) -> a x", a=128),
                    in_=src[b, 8:S, :].rearrange("r d -> (r d)").rearrange("(a x) -> a x", a=128)),
                      writes=[key])
                out_keys.append(key)
        for l in range(2 if not DBG.get("nosprow") else 0):
            key = ("o", "sprow", l)
            P.dma("sp", ("sprow", l), lambda e, l=l: e.dma_start(out=pool_s[l, :, 0:7, :], in_=sprow[l]), writes=[key])
            out_keys.append(key)
        P.op("dve", lambda e: e.memset(ONES[:], 1.0), writes=["ONES"])
        P.op("dve", lambda e: e.memset(EPSV[:], EPS), writes=["EPSV"])
        for i in range(NVS):
            P.op("pool", lambda e, i=i: e.memset(VS[i][:].rearrange("p (c x) -> p c x", x=256)[:, :, 64:192], 1.0), writes=[("vsj", i, 0), ("vsj", i, 1)])
        P.op("pool", lambda e: e.memset(VsN[:].rearrange("p (c x) -> p c x", x=256)[:, :, 64:192], 1.0), writes=["VsN"])
        for l in range(4):
            for idx in (l * 6 + 1, l * 6 + 5):
                P.op("dve", lambda e, idx=idx: e.tensor_scalar(out=GV[:, :, idx:idx + 1], in0=GV[:, :, idx:idx + 1], scalar1=0.5,
                                                               scalar2=None, op0=ALU.mult), reads=["GV"], writes=["GV"])

        def w2k_load(src):
            s = nxt("w2k", NW2K)
            P.dma("pool", ("w2k", s), lambda e, s=s, src=src: e.dma_start(out=W2K[s][:], in_=src.rearrange("(c p) m -> p c m", p=128)),
                  writes=[("w2k", s)])
            return s

        def wd_load(src, kind="d"):
            s = nxt("wd", 2)
            if kind == "d":
                for hf in range(2):
                    P.dma("pool", ("wd", s, hf), lambda e, s=s, src=src, hf=hf: e.dma_start(
                        out=WD[s][:, 11 * hf:11 * hf + 11, :],
                        in_=src[1408 * hf:1408 * hf + 1408, :].rearrange("(j p) m -> p j m", p=128)), writes=[("wd", s, hf)])
            else:
                P.dma("pool", ("wd", s, 0), lambda e, s=s, src=src: e.dma_start(
                    out=WD[s][:].rearrange("p j m -> p (j m)")[:, 0:2048].rearrange("p (g k n) -> p g k n", g=4, k=2),
                    in_=src.rearrange("g (k p) n -> p g k n", p=128)), writes=[("wd", s, 0), ("wd", s, 1)])
            return s

        def mm_acc(b, col, n, pairs, reads, M=128, extra_writes=(), first=True, fin=True):
            def f(e, b=b, col=col, n=n, pairs=pairs, M=M, first=first, fin=fin):
                last = len(pairs) - 1
                for i, (l, r) in enumerate(pairs):
                    ins = e.matmul(PBX[b][0:M, col:col + n], lhsT=l, rhs=r, start=(first and i == 0), stop=(fin and i == last))
                return ins
            return P.op("pe", f, reads=reads, writes=[bk(b)] + list(extra_writes))

        def stats_add(src_ap, n, si, first, last, reads, ser=()):
            i = nxt("sq", 2)
            P.op("act", lambda e, i=i, src_ap=src_ap, n=n: e.activation(out=SQ[i][:, 0:n], in_=src_ap, func=AF.Square),
                 reads=reads, writes=[("sq", i)] + list(ser))
            P.op("pe", lambda e, i=i, n=n, si=si, first=first, last=last: e.matmul(
                PST[si][:, 0:n], lhsT=ONES[:], rhs=SQ[i][:, 0:n], start=first, stop=last),
                reads=[("sq", i), "ONES"], writes=[("pst", si)])

        def stats_sq(src_ap, n, reads, ser=()):
            i = nxt("sq", 2)
            P.op("act", lambda e, i=i, src_ap=src_ap, n=n: e.activation(out=SQ[i][:, 0:n], in_=src_ap, func=AF.Square),
                 reads=reads, writes=[("sq", i)] + list(ser))
            return i

        def stats_mm(i, n, si, first, last):
            P.op("pe", lambda e, i=i, n=n, si=si, first=first, last=last: e.matmul(
                PST[si][:, 0:n], lhsT=ONES[:], rhs=SQ[i][:, 0:n], start=first, stop=last),
                reads=[("sq", i), "ONES"], writes=[("pst", si)])

        def stats_finish(n, si):
            P.op("act", lambda e, n=n, si=si: e.activation(out=TMPS[si][:, 0:n], in_=PST[si][:, 0:n], func=AF.Ln, scale=1.0 / D, bias=EPSV[:, 0:1]),
                 reads=[("pst", si), "EPSV"], writes=[("tmps", si)])
            P.op("act", lambda e, n=n, si=si: e.activation(out=RSTD[si][:, 0:n], in_=TMPS[si][:, 0:n], func=AF.Exp, scale=-0.5),
                 reads=[("tmps", si)], writes=[("rstd", si)])

        def prenorm(gidx, segs, a2=False):
            for seg in segs:
                h0, n, l0, sid, si = seg["h0"], seg["n"], seg["l0"], seg["id"], seg["si"]
                for c in range(8):
                    stats_add(H[:, c, h0:h0 + n], n, si, c == 0, c == 7, [("h", c, sid)])
                stats_finish(n, si)
                for c in range(8):
                    dst = A2[:, c, 0:n] if a2 else Ab[:, c, l0:l0 + n]
                    wk = [("A2", c)] if a2 else kA(c)
                    P.op("dve", lambda e, c=c, h0=h0, n=n, si=si, dst=dst: e.scalar_tensor_tensor(
                        out=dst, in0=H[:, c, h0:h0 + n], scalar=GV[:, c, gidx:gidx + 1], in1=RSTD[si][:, 0:n],
                        op0=ALU.mult, op1=ALU.mult), reads=[("h", c, sid), ("rstd", si), "GV"], writes=wk)

        def postnorm(gidx, segs):
            for seg in segs:
                h0, n, l0, sid, si = seg["h0"], seg["n"], seg["l0"], seg["id"], seg["si"]
                stats_finish(n, si)
                for m in range(8):
                    P.op("dve", lambda e, m=m, n=n, l0=l0, si=si: e.scalar_tensor_tensor(
                        out=Fb[:, m, l0:l0 + n], in0=Fb[:, m, l0:l0 + n], scalar=GV[:, m, gidx:gidx + 1], in1=RSTD[si][:, 0:n],
                        op0=ALU.mult, op1=ALU.mult), reads=kF(m) + [("rstd", si), "GV"], writes=kF(m))
                    P.op("dve", lambda e, m=m, h0=h0, n=n, l0=l0: e.tensor_tensor(
                        out=H[:, m, h0:h0 + n], in0=H[:, m, h0:h0 + n], in1=Fb[:, m, l0:l0 + n], op=ALU.add),
                        reads=kF(m) + [("h", m, sid)], writes=[("h", m, sid)])

        def out_proj(wsrc_fn, rhs_chunks, segs):
            ss = [w2k_load(wsrc_fn(0))]
            pend = None
            for m in range(8):
                if m + 1 < 8:
                    ss.append(w2k_load(wsrc_fn(m + 1)))
                s = ss[m]
                for seg in segs:
                    n, l0, si = seg["n"], seg["l0"], seg["si"]
                    b = bank()
                    mm_acc(b, 0, n, [(W2K[s][:, k, :], BIG[:, rhs_chunks[k], l0:l0 + n]) for k in range(8)],
                           [("w2k", s)] + [("B", rhs_chunks[k]) for k in range(8)])
                    if pend is not None:
                        stats_mm(*pend)
                    sqi = stats_sq(PB[b][:, 0:n], n, [("ps", b)], ser=[("psr", b)])
                    pend = (sqi, n, si, m == 0, m == 7)
                    P.op("dve", lambda e, m=m, b=b, n=n, l0=l0: e.tensor_copy(out=Fb[:, m, l0:l0 + n], in_=PB[b][:, 0:n]),
                         reads=[("ps", b), ("psr", b)], writes=kF(m))
            stats_mm(*pend)

        carry = {}

        def ffn(layer, which, p, pre=True, hoist=False):
            segs = SEGS[p]
            gpre = layer * 6 + (0 if which == 0 else 4)
            if pre:
                prenorm(gpre, segs[0:1], a2=True)
            if len(segs) > 1:
                prenorm(gpre, segs[1:2])

            def asrc(k, seg):
                if seg["n"] == 512:
                    return A2[:, k, 0:512], [("A2", k)]
                return Ab[:, k, seg["l0"]:seg["l0"] + seg["n"]], kA(k)
            wg, wu, wdn = wgate[layer, which], wup[layer, which], wdown[layer, which]
            pend = []

            def issue(j):
                pend.append((w2k_load(wg[:, j * 128:(j + 1) * 128]), w2k_load(wu[:, j * 128:(j + 1) * 128])))
            if "p0" in carry:
                pend.append(carry.pop("p0"))
            else:
                issue(0)
            dslots = [wd_load(wdn[:, 0:128])]
            for j in range(NJ):
                if j + 1 < NJ:
                    issue(j + 1)
                sg_, su_ = pend[j]
                for seg in segs:
                    n, l0 = seg["n"], seg["l0"]
                    if n == 512:
                        bg, bu, cg, cu = bank(), bank(), 0, 0
                    else:
                        bg = bu = NB
                        cg, cu = 0, 32
                    ards = [x for k in range(8) for x in asrc(k, seg)[1]]
                    mm_acc(bg, cg, n, [(W2K[sg_][:, k, :], asrc(k, seg)[0]) for k in range(8)], [("w2k", sg_)] + ards)
                    mm_acc(bu, cu, n, [(W2K[su_][:, k, :], asrc(k, seg)[0]) for k in range(8)], [("w2k", su_)] + ards)
                    i = nxt("sg", 2)
                    P.op("act", lambda e, i=i, bg=bg, cg=cg, n=n: e.activation(out=SG[i][:, 0:n], in_=PBX[bg][:, cg:cg + n], func=AF.Silu),
                         reads=[bk(bg)], writes=[("sg", i)])
                    P.op("dve", lambda e, i=i, j=j, bu=bu, cu=cu, n=n, l0=l0: e.tensor_tensor(
                        out=BIG[:, j, l0:l0 + n], in0=SG[i][:, 0:n], in1=PBX[bu][:, cu:cu + n], op=ALU.mult),
                        reads=[("sg", i), bk(bu)], writes=[("B", j)])
            if hoist:
                prenorm(gpre, SEGS[p + 1][0:1], a2=True)
                if not (layer == 2 and which == 0):
                    carry["p0"] = (w2k_load(wg[:, 0:128]), w2k_load(wu[:, 0:128]))
            pend2 = None
            for m in range(8):
                if m + 1 < 8:
                    dslots.append(wd_load(wdn[:, (m + 1) * 128:(m + 2) * 128]))
                sd = dslots[m]
                for seg in segs:
                    n, l0, si = seg["n"], seg["l0"], seg["si"]
                    b = bank()
                    mm_acc(b, 0, n, [(WD[sd][:, j, :], BIG[:, j, l0:l0 + n]) for j in range(11)],
                           [("wd", sd, 0)] + [("B", j) for j in range(11)], fin=False)
                    mm_acc(b, 0, n, [(WD[sd][:, j, :], BIG[:, j, l0:l0 + n]) for j in range(11, NJ)],
                           [("wd", sd, 1)] + [("B", j) for j in range(11, NJ)], first=False)
                    if pend2 is not None:
                        stats_mm(*pend2)
                    sqi = stats_sq(PB[b][:, 0:n], n, [("ps", b)], ser=[("psr", b)])
                    pend2 = (sqi, n, si, m == 0, m == 7)
                    P.op("dve", lambda e, m=m, b=b, n=n, l0=l0: e.tensor_copy(out=Fb[:, m, l0:l0 + n], in_=PB[b][:, 0:n]),
                         reads=[("ps", b), ("psr", b)], writes=kF(m))
            stats_mm(*pend2)
            postnorm(gpre + 1, segs)

        def pool_mixer(layer, p):
            segs = SEGS[p]
            prenorm(layer * 6 + 2, segs)
            Ukeys = [("U", m) for m in range(8)]
            if p == 0:
                P.op("dve", lambda e: e.memset(U[:, :, 0:16], 0.0), reads=Ukeys, writes=Ukeys)
                for c in range(8):
                    P.dma("sp", ("ld_us", c), lambda e, c=c: e.dma_start(
                        out=Us[:, c, :, 0:15], in_=spT[layer, c * 128:(c + 1) * 128, :].rearrange("p (b r) -> p b r", b=4)),
                        writes=[("Us", c)])
            else:
                P.op("dve", lambda e: e.tensor_copy(out=U[:, :, 0:16], in_=U[:, :, 512:528]), reads=Ukeys, writes=Ukeys)
            win = pwin[layer]
            ss = [w2k_load(win[:, 0:128])]
            for m in range(8):
                if m + 1 < 8:
                    ss.append(w2k_load(win[:, (m + 1) * 128:(m + 2) * 128]))
                s = ss[m]
                for seg in segs:
                    n, l0 = seg["n"], seg["l0"]
                    b = bank()
                    mm_acc(b, 0, n, [(W2K[s][:, k, :], Ab[:, k, l0:l0 + n]) for k in range(8)],
                           [("w2k", s)] + [x for k in range(8) for x in kA(k)])
                    if n == 512:
                        P.op("dve", lambda e, m=m, b=b: e.tensor_copy(out=U[:, m, 16:528], in_=PB[b][:, 0:512]),
                             reads=[("ps", b)], writes=[("U", m)])
                    else:
                        P.op("dve", lambda e, m=m, b=b: e.tensor_copy(
                            out=Us[:, m, :, 15:23], in_=PB[b][:, 0:32].rearrange("p (b t) -> p b t", b=4)),
                            reads=[("ps", b)], writes=[("Us", m)])
                if p == 3:
                    b = bank()
                    mm_acc(b, 0, 128, [(Ab[:, k, 497:544], W2K[s][:, k, :]) for k in range(8)],
                           [("w2k", s)] + [x for k in range(8) for x in kA(k)], M=47)
                    P.op("dve", lambda e, m=m, b=b: e.tensor_copy(out=TMP_tok[0:47, m * 128:(m + 1) * 128], in_=PB[b][0:47, 0:128]),
                         reads=[("ps", b)], writes=["TMPtok"])
            if p == 3:
                key = ("o", "pool_p", layer)
                P.dma("sp", ("pool_p", layer), lambda e: e.dma_start(out=pool_p[layer], in_=TMP_tok[0:15, :]), reads=["TMPtok"], writes=[key])
                out_keys.append(key)
                key = ("o", "pool_s", layer)
                P.dma("sp", ("pool_s", layer), lambda e: e.dma_start(out=pool_s[layer, :, 7:15, :], in_=TMP_tok[15:47, :]),
                      reads=["TMPtok"], writes=[key])
                out_keys.append(key)
            for m in range(8):
                gi = m // 2
                w = WINS[gi]
                for seg in segs:
                    if seg["n"] == 512:
                        src = U[:, m, :]
                        bufs = (PSa, PSb)
                        sh = 1
                        cur = src
                        for lv in range(gi + 1):
                            dst = bufs[lv % 2]
                            lo = 2 * sh - 1
                            P.op("dve", lambda e, dst=dst, cur=cur, lo=lo, sh=sh: e.tensor_tensor(
                                out=dst[:, lo:528], in0=cur[:, lo:528], in1=cur[:, lo - sh:528 - sh], op=ALU.add),
                                reads=[("U", m), ("psc", 0), ("psc", 1)], writes=[("psc", lv % 2)])
                            cur = dst
                            sh *= 2
                        P.op("dve", lambda e, m=m, cur=cur, w=w: e.scalar_tensor_tensor(
                            out=BIG[:, m, 0:512], in0=cur[:, 16:528], scalar=1.0 / w, in1=U[:, m, 16:528],
                            op0=ALU.mult, op1=ALU.subtract), reads=[("psc", gi % 2), ("U", m)], writes=[("B", m)])
                        if p == 0:
                            P.op("dve", lambda e, cur=cur, gi=gi: e.tensor_tensor(out=T16, in0=cur[:, 16:32], in1=RCT[:, gi, :], op=ALU.mult),
                                 reads=[("psc", gi % 2), "RCT"], writes=["T16"])
                            P.op("dve", lambda e, m=m: e.tensor_tensor(out=BIG[:, m, 0:16], in0=T16, in1=U[:, m, 16:32], op=ALU.subtract),
                                 reads=["T16", ("U", m)], writes=[("B", m)])
                    else:
                        src = Us[:, m, :, :]
                        bufs = (SSa, SSb)
                        sh = 1
                        cur = src
                        for lv in range(gi + 1):
                            dst = bufs[lv % 2]
                            lo = 2 * sh - 1
                            P.op("dve", lambda e, dst=dst, cur=cur, lo=lo, sh=sh: e.tensor_tensor(
                                out=dst[:, :, lo:23], in0=cur[:, :, lo:23], in1=cur[:, :, lo - sh:23 - sh], op=ALU.add),
                                reads=[("Us", m), ("ssc", 0), ("ssc", 1)], writes=[("ssc", lv % 2)])
                            cur = dst
                            sh *= 2
                        P.op("dve", lambda e, m=m, cur=cur, w=w: e.scalar_tensor_tensor(
                            out=BIG[:, m, 512:544].rearrange("p (b t) -> p b t", b=4), in0=cur[:, :, 15:23], scalar=1.0 / w,
                            in1=Us[:, m, :, 15:23], op0=ALU.mult, op1=ALU.subtract),
                            reads=[("ssc", gi % 2), ("Us", m)], writes=[("B", m)])
            sgw = wd_load(pwgrp[layer], kind="g")
            WG = WD[sgw][:].rearrange("p j m -> p (j m)")[:, 0:2048].rearrange("p (g k n) -> p g k n", g=4, k=2)
            for gi in range(4):
                for mo in range(2):
                    mch = 2 * gi + mo
                    for seg in segs:
                        n, l0 = seg["n"], seg["l0"]
                        b = bank()
                        mm_acc(b, 0, n, [(WG[:, gi, ki, mo * 128:(mo + 1) * 128], BIG[:, 2 * gi + ki, l0:l0 + n]) for ki in range(2)],
                               [("wd", sgw, 0), ("wd", sgw, 1), ("B", 2 * gi), ("B", 2 * gi + 1)])
                        P.op("dve", lambda e, mch=mch, b=b, n=n, l0=l0: e.tensor_scalar(
                            out=BIG[:, 8 + mch, l0:l0 + n], in0=PB[b][:, 0:n], scalar1=GV[:, mch, 25 + layer:26 + layer], scalar2=None,
                            op0=ALU.mult), reads=[("ps", b), "GV"], writes=[("B", 8 + mch)])
            wout = pwout[layer]
            out_proj(lambda m: wout[:, m * 128:(m + 1) * 128], [8 + k for k in range(8)], segs)
            postnorm(layer * 6 + 3, segs)

        def kv_phase(p):
            segs = SEGS[p]
            prenorm(24, segs)
            h0 = 512 * p
            areads = [x for k in range(8) for x in kA(k)]
            for which, wsrc, dst in (("k", wk_d, k_p), ("v", wv_d, v_p)):
                ss = [w2k_load(wsrc[:, 0:128])]
                for m in range(8):
                    if m + 1 < 8:
                        ss.append(w2k_load(wsrc[:, (m + 1) * 128:(m + 2) * 128]))
                    s = ss[m]
                    if which == "k":
                        b = bank()
                        mm_acc(b, 0, 512, [(W2K[s][:, k, :], Ab[:, k, 0:512]) for k in range(8)], [("w2k", s)] + areads)
                        P.op("act", lambda e, m=m, b=b, h0=h0: e.activation(out=KT[:, m, h0:h0 + 512], in_=PB[b][:, 0:512], func=AF.Copy),
                             reads=[("ps", b)], writes=[("KT", m, p)])
                        if p == 3:
                            b = bank()
                            mm_acc(b, 0, 32, [(W2K[s][:, k, :], Ab[:, k, 512:544]) for k in range(8)], [("w2k", s)] + areads)
                            P.op("act", lambda e, m=m, b=b: e.activation(out=KTs[:, m, :], in_=PB[b][:, 0:32], func=AF.Copy),
                                 reads=[("ps", b)], writes=["KTs"])
                    b = bank()
                    for tt in range(4):
                        mm_acc(b, tt * 128, 128, [(Ab[:, k, tt * 128:(tt + 1) * 128], W2K[s][:, k, :]) for k in range(8)],
                               [("w2k", s)] + areads)
                    i = nxt("ks", 2)
                    P.op("dve", lambda e, i=i, b=b: e.tensor_copy(out=KS[i][:].rearrange("p t x -> p (t x)"), in_=PB[b][:, 0:512]),
                         reads=[("ps", b)], writes=[("ks", i)])
                    key = ("o", which + "p", p, m)
                    P.dma("sp", ("ks", i), lambda e, i=i, m=m, dst=dst, h0=h0: e.dma_start(
                        out=dst[h0:h0 + 512, m * 128:(m + 1) * 128].rearrange("(t p) x -> p t x", p=128), in_=KS[i][:]),
                        reads=[("ks", i)], writes=[key])
                    out_keys.append(key)
                    if p == 3:
                        b = bank()
                        mm_acc(b, 0, 128, [(Ab[:, k, 512:544], W2K[s][:, k, :]) for k in range(8)], [("w2k", s)] + areads, M=32)
                        i = nxt("ks", 2)
                        P.op("dve", lambda e, i=i, b=b: e.tensor_copy(out=KS[i][0:32, 0, :], in_=PB[b][0:32, 0:128]),
                             reads=[("ps", b)], writes=[("ks", i)])
                        dsts = k_s if which == "k" else v_s
                        key = ("o", which + "s_new", m)
                        P.dma("sp", ("ks", i), lambda e, i=i, m=m, dsts=dsts: e.dma_start(
                            out=dsts[:, S - 8:S, m * 128:(m + 1) * 128], in_=KS[i][0:32, 0, :]), reads=[("ks", i)], writes=[key])
                        out_keys.append(key)
                        if which == "v":
                            P.op("act", lambda e, m=m, b=b: e.activation(
                                out=VsN[0:32, 256 * m:256 * m + 256].rearrange("p (j x) -> p j x", x=64)[:, 0:4:3, :],
                                in_=PB[b][0:32, 0:128].rearrange("p (j x) -> p j x", x=64), func=AF.Copy), reads=[("ps", b), ("ks", i)], writes=["VsN"])

        def vaug(vt, h, nk):
            hl = h % 8
            return vt[0:nk, 128 * hl:128 * hl + 128]

        def vs_load(src_rows, nk, reads, half):
            s = nxt("vs", NVS)
            for j in range(2):
                P.dma("pool", ("vs", s, j), lambda e, s=s, src_rows=src_rows, nk=nk, j=j, half=half: e.dma_start(
                    out=VS[s][0:nk, :].rearrange("p (c x) -> p c x", x=256)[:, :, 192 * j:192 * j + 64],
                    in_=src_rows[:, 512 * half:512 * half + 512].rearrange("p (c x) -> p c x", x=128)[:, :, 64 * j:64 * j + 64]),
                    reads=reads, writes=[("vsj", s, j)])
            return s

        def sl(start, cnt, step):
            return slice(start, start + step * (cnt - 1) + 1, step)

        def attn_units(p):
            h0 = 512 * p
            units = []
            for qb in range(4):
                T = h0 + 128 * qb
                blocks = []
                if T > 0:
                    blocks.append((sl(T - 128, 128, 1), sl(T - 128, 128, 1), 128, {(T - 128) // 512}))
                blocks.append((sl(T, 128, 1), sl(T, 128, 1), 128, {p}))
                units.append((0, 128, sl(128 * qb, 128, 1), blocks, (0 if T > 0 else 128)))
            for r in range(4):
                blocks = []
                if p > 0:
                    blocks.append((sl(h0 - 512 + r, 128, 4), sl(h0 - 512 + r, 128, 4), 128, {p - 1}))
                blocks.append((sl(h0 + r, 128, 4), sl(h0 + r, 128, 4), 128, {p}))
                units.append((1, 128, sl(r, 128, 4), blocks, (0 if p > 0 else 128)))
            nk = 32 * (p + 1)
            for r in range(16):
                blocks = [(sl(r, nk, 16), sl(r, nk, 16), nk, set(range(p + 1)))]
                units.append((2, 32, sl(r, 32, 16), blocks, 256 + 32 * p))
            return units

        def attn_mixer(layer, p):
            segs = SEGS[p]
            jl = layer - 2
            prenorm(layer * 6 + 2, segs)
            areads = [x for k in range(8) for x in kA(k)]
            wq = wq_d[jl]
            cols = [(c, g) for c in range(8) for g in range(3)]
            ss = [w2k_load(wq[:, cols[0][1] * D + cols[0][0] * 128:cols[0][1] * D + cols[0][0] * 128 + 128])]
            for i, (c, g) in enumerate(cols):
                if i + 1 < len(cols):
                    c2, g2 = cols[i + 1]
                    ss.append(w2k_load(wq[:, g2 * D + c2 * 128:g2 * D + c2 * 128 + 128]))
                s = ss[i]
                for seg in segs:
                    n, l0 = seg["n"], seg["l0"]
                    b = bank()
                    mm_acc(b, 0, n, [(W2K[s][:, k, :], Ab[:, k, l0:l0 + n]) for k in range(8)], [("w2k", s)] + areads)
                    P.op("act", lambda e, c=c, g=g, b=b, n=n, l0=l0: e.activation(out=BIG[:, 3 * c + g, l0:l0 + n], in_=PB[b][:, 0:n], func=AF.Copy),
                         reads=[("ps", b)], writes=[("B", 3 * c + g)])
            units = attn_units(p)
            LAG = 3
            for hf in range(2):
                items = [(ui, hl) for ui in range(len(units)) for hl in range(8)]
                uvs = {}

                def stage_a(ui, hl):
                    (g, nq, qsl, blocks, nscol) = units[ui]
                    if ui not in uvs:
                        uvs[ui] = [vs_load(v_p[vrows, :], nk, [("o", "vp", pp, m) for pp in kps for m in range(8)], hf)
                                   for (ksl, vrows, nk, kps) in blocks]
                    nb = len(blocks)
                    h = 8 * hf + hl
                    c, r0 = h // 2, 64 * (h % 2)
                    bs = bank()
                    for bi, (ksl, vrows, nk, kps) in enumerate(blocks):
                        mm_acc(bs, bi * nq, nq, [(KT[r0:r0 + 64, c, ksl], BIG[r0:r0 + 64, 3 * c + g, qsl])],
                               [("KT", c, pp) for pp in kps] + [("B", 3 * c + g)], M=nk)
                    nkm = blocks[0][2]
                    i = nxt("tt", 4)
                    cf = -slope(h) * DILS[g] / SCALE
                    P.op("dve", lambda e, i=i, bs=bs, nkm=nkm, w=nb * nq, nscol=nscol, cf=cf: e.scalar_tensor_tensor(
                        out=TT[i][0:nkm, 0:w], in0=NST[0:nkm, nscol:nscol + w], scalar=cf, in1=PB[bs][0:nkm, 0:w],
                        op0=ALU.mult, op1=ALU.add), reads=[("ps", bs), "NST"], writes=[("tt", i)])
                    P.op("act", lambda e, i=i, nkm=nkm, w=nb * nq: e.activation(out=PT[i][0:nkm, 0:w], in_=TT[i][0:nkm, 0:w], func=AF.Exp,
                                                                              scale=SCALE), reads=[("tt", i)], writes=[("pt", i)])
                    return i

                def stage_b(ui, hl, i):
                    (g, nq, qsl, blocks, nscol) = units[ui]
                    vsl = uvs[ui]
                    nb = len(blocks)
                    h = 8 * hf + hl
                    bo = bank()
                    mm_acc(bo, 0, nq, [(vaug(VS[vsl[bi]], h, blocks[bi][2]), PT[i][0:blocks[bi][2], bi * nq:(bi + 1) * nq]) for bi in range(nb)],
                           [("pt", i)] + [("vsj", s_, j_) for s_ in vsl for j_ in range(2)])
                    if g == 0:
                        P.op("act", lambda e, hl=hl, bo=bo, qsl=qsl, nq=nq: e.activation(out=Fb[:, hl, qsl], in_=PB[bo][:, 0:nq], func=AF.Copy),
                             reads=[("ps", bo)], writes=kF(hl))
                    else:
                        P.op("dve", lambda e, hl=hl, bo=bo, qsl=qsl, nq=nq: e.tensor_tensor(
                            out=Fb[:, hl, qsl], in0=Fb[:, hl, qsl], in1=PB[bo][:, 0:nq], op=ALU.add),
                            reads=[("ps", bo)] + kF(hl), writes=kF(hl))

                st_i = {}
                for idx in range(len(items) + LAG):
                    if idx < len(items):
                        st_i[idx] = stage_a(*items[idx])
                    if idx - LAG >= 0:
                        stage_b(*items[idx - LAG], st_i.pop(idx - LAG))
                for hl in range(8):
                    h = 8 * hf + hl
                    c, r0 = h // 2, 64 * (h % 2)
                    b = bank()
                    P.op("pe", lambda e, b=b, hl=hl: e.matmul(PB[b][:, 0:512], lhsT=PSW[:], rhs=Fb[:, hl, 0:512], start=True, stop=True),
                         reads=kF(hl) + ["PSW"], writes=[("ps", b)])
                    P.op("act", lambda e, b=b, r0=r0: e.activation(out=LNB[r0:r0 + 64, :], in_=PB[b][r0:r0 + 64, 0:512], func=AF.Ln),
                         reads=[("ps", b)], writes=[("tmps", 0)])
                    P.op("act", lambda e, r0=r0: e.activation(out=REC[r0:r0 + 64, :], in_=LNB[r0:r0 + 64, :], func=AF.Exp, scale=-1.0),
                         reads=[("tmps", 0)], writes=[("tmps", 1)])
                    P.op("dve", lambda e, c=c, hl=hl, r0=r0: e.tensor_tensor(
                        out=BIG[r0:r0 + 64, 3 * c, 0:512], in0=Fb[r0:r0 + 64, hl, 0:512], in1=REC[r0:r0 + 64, :], op=ALU.mult),
                        reads=kF(hl) + [("tmps", 1)], writes=[("B", 3 * c)])
            if p == 3 and not DBG.get("nosample"):
                sample_attn()
            wo = wo_d[jl]
            out_proj(lambda m: wo[:, m * 128:(m + 1) * 128], [3 * k for k in range(8)], segs)
            postnorm(layer * 6 + 3, segs)

        def sample_attn():
            tiles = [(0, "c", 1920, 0)] + [(1, "c", 1536 + 128 * i, 1 + i) for i in range(4)] + [(2, "s", r, 5 + r) for r in range(8)]
            sa = DBG.get("sa", {})
            QZ = Fb[:].rearrange("p c n -> p (c n)")[:, 0:768].bitcast(BF16).rearrange("p (g h t) -> p g h t", g=3, h=16)
            qzk = kF(0) + kF(1)
            P.op("pool", lambda e: e.memset(QZ, 0.0), writes=qzk)
            for g in range(3):
                for hh in range(2):
                    r0 = 64 * hh
                    P.op("dve", lambda e, g=g, hh=hh, r0=r0: e.tensor_copy(out=QZ[r0:r0 + 64, g, hh:16:2, :], in_=BIG[r0:r0 + 64, g:24:3, 512:544]),
                         reads=[("B", 3 * c + g) for c in range(8)] + qzk, writes=qzk)
            for b in range(sa.get("nb", 4)):
                first = [True]
                qc = 512 + 8 * b

                def do_tile(g, nk, kT_fn, kreads, vt, vreads, tab_src):
                    i = nxt("sbi", 1)
                    P.dma("sp", ("sbi", i), lambda e, i=i, tab_src=tab_src, nk=nk: e.dma_start(out=SBI[i][0:nk, :], in_=tab_src),
                          writes=[("sbi", i)])
                    bs = bank()

                    def fs(e, bs=bs, nk=nk, g=g, kT_fn=kT_fn, b=b):
                        for h in range(16):
                            c, r0 = h // 2, 64 * (h % 2)
                            ins = e.matmul(PB[bs][0:nk, h * 8:(h + 1) * 8], lhsT=kT_fn(c, r0), rhs=QZ[:, g, h, 8 * b:8 * b + 8],
                                           start=(h == 0), stop=(h == 15), skip_group_check=True)
                        return ins
                    steps = sa.get("steps", 31)
                    if steps & 2:
                        P.op("pe", fs, reads=kreads + qzk, writes=[("ps", bs)])
                    ti = nxt("tt", 4)
                    if steps & 4:
                      P.op("dve", lambda e, ti=ti, bs=bs, nk=nk, i=i: e.scalar_tensor_tensor(
                        out=TT[ti][0:nk, 0:128], in0=PB[bs][0:nk, 0:128], scalar=SCALE, in1=SBI[i][0:nk, :], op0=ALU.mult, op1=ALU.add),
                        reads=[("ps", bs), ("sbi", i)], writes=[("tt", ti)])
                    if steps & 8:
                      P.op("act", lambda e, ti=ti, nk=nk: e.activation(out=PT[ti][0:nk, 0:128], in_=TT[ti][0:nk, 0:128], func=AF.Exp),
                         reads=[("tt", ti)], writes=[("pt", ti)])
                    fl = first[0]
                    first[0] = False

                    def fo(e, nk=nk, vt=vt, ti=ti, fl=fl):
                        for h in range(16):
                            ins = e.matmul(PST[1][:, h * 8:(h + 1) * 8], lhsT=vaug(vt[h // 8], h, nk), rhs=PT[ti][0:nk, h * 8:(h + 1) * 8],
                                           start=(fl and h == 0), stop=False, skip_group_check=True)
                        return ins
                    if steps & 16:
                        P.op("pe", fo, reads=[("pt", ti)] + vreads, writes=[("pst", 1)])

                for (g, kind, r, tab) in (tiles if sa.get("cache", True) else []):
                    if kind not in sa.get("kinds", "cs"):
                        continue
                    rows = sl(r, 128, 1) if kind == "c" else sl(r, 128, 16)
                    ki = nxt("kr", 2)
                    P.dma("pool", ("kr", ki), lambda e, ki=ki, rows=rows, b=b: e.dma_start(out=KR[ki], in_=ck[b, rows, :]), writes=[("sg", ki)])

                    def ftr(e, ki=ki):
                        for c in range(8):
                            ins = e.transpose(PTR[:, c * 128:(c + 1) * 128], KR[ki][:, c * 128:(c + 1) * 128], IDN[:])
                        return ins
                    P.op("pe", ftr, reads=[("sg", ki), "IDN"], writes=["ptr"])
                    P.op("act", lambda e, ki=ki: e.activation(out=KsT[ki].rearrange("p c n -> p (c n)"), in_=PTR[:, :], func=AF.Copy),
                         reads=["ptr"], writes=[("ks", ki)])
                    vsa = vs_load(cv[b, rows, :], 128, [], 0)
                    vsb = vs_load(cv[b, rows, :], 128, [], 1)
                    do_tile(g, 128, lambda c, r0, ki=ki: KsT[ki][:, c, :], [("ks", ki)], (VS[vsa], VS[vsb]),
                            [("vsj", v_, j_) for v_ in (vsa, vsb) for j_ in range(2)], sbias_d[tab])
                for g in range(3 if sa.get("new", True) else 0):
                    do_tile(g, 32, lambda c, r0: KTs[:, c, :], ["KTs"], (VsN[:, 0:1024], VsN[:, 1024:2048]), ["VsN"], sbias_d[13 + 4 * g + b, 0:32, :])
                if not sa.get("norm", True):
                    continue
                P.op("dve", lambda e: e.tensor_copy(out=S1[:], in_=PST[1][:, 0:128]), reads=[("pst", 1)], writes=["S1"])
                b2 = bank()
                P.op("pe", lambda e, b2=b2: e.matmul(PB[b2][:, 0:128], lhsT=PSW[:], rhs=S1[:], start=True, stop=True),
                     reads=["S1", "PSW"], writes=[("ps", b2)])
                for hh in range(2):
                    r0 = 64 * hh
                    pv = PB[b2][r0:r0 + 64, 0:128].rearrange("p (c x t) -> p c x t", c=8, x=2)[:, :, hh, :]
                    lv = LNB[r0:r0 + 64, 0:64].rearrange("p (c t) -> p c t", c=8)
                    rv = REC[r0:r0 + 64, 0:64].rearrange("p (c t) -> p c t", c=8)
                    sv = S1[r0:r0 + 64, :].rearrange("p (c x t) -> p c x t", c=8, x=2)[:, :, hh, :]
                    P.op("act", lambda e, pv=pv, lv=lv: e.activation(out=lv, in_=pv, func=AF.Ln), reads=[("ps", b2)], writes=[("tmps", 0)])
                    P.op("act", lambda e, lv=lv, rv=rv: e.activation(out=rv, in_=lv, func=AF.Exp, scale=-1.0), reads=[("tmps", 0)], writes=[("tmps", 1)])
                    P.op("dve", lambda e, sv=sv, rv=rv, r0=r0, qc=qc: e.tensor_tensor(
                        out=BIG[r0:r0 + 64, 0:24:3, qc:qc + 8], in0=sv, in1=rv, op=ALU.mult),
                        reads=["S1", ("tmps", 1)], writes=[("B", 3 * c) for c in range(8)])

        def body():
            for layer in range(4):
                if layer == 2:
                    P.barrier()
                for p in range(4):
                    if layer == 2:
                        kv_phase(p)
                    ffn(layer, 0, p, pre=(p == 0), hoist=(p < 3))
                    if stop == ("ffn0", layer, p):
                        return
                for p in range(4):
                    if layer < 2:
                        pool_mixer(layer, p)
                    else:
                        attn_mixer(layer, p)
                    if stop == ("mix", layer, p):
                        return
                for p in range(4):
                    ffn(layer, 1, p, pre=(p == 0), hoist=(p < 3))
                    if stop == ("ffn1", layer, p):
                        return
        body()
        key = ("o", "yT")
        P.dma("sp", "st_y", lambda e: e.dma_start(out=yT.rearrange("(c p) n -> p c n", p=128), in_=H[:]),
              reads=[("h", c, sid) for c in range(8) for sid in (0, 1, 2, 3, "s")], writes=[key])
        out_keys.append(key)
        P.op("sp", None, reads=out_keys)
        P.emit()
    return nc


def host_tables():
    ident = np.eye(128, dtype=np.float32)
    psw = np.zeros((128, 128), np.float32)
    for i in range(128):
        psw[i, (i + 64) % 128] = 1.0
    k = np.arange(128)[:, None]
    q = np.arange(128)[None, :]
    ns_prev = np.where(k >= q, q + 128 - k, BIGN).astype(np.float32)
    ns_cur = np.where(q >= k, q - k, BIGN).astype(np.float32)
    ns2 = np.zeros((128, 128), np.float32)
    for p in range(4):
        j = np.arange(32)[None, :]
        st = 32 * p + j - k
        ns2[:, 32 * p:32 * p + 32] = np.where(st >= 0, st, BIGN)
    nstab = np.concatenate([ns_prev, ns_cur, ns2], axis=1).astype(np.float32)
    rct = np.zeros((128, 4, 16), np.float32)
    for gi, w in enumerate(WINS):
        rct[:, gi, :] = 1.0 / np.minimum(np.arange(16) + 1, w)
    rctab = rct.reshape(128, 64)
    sl_h = np.array([slope(h) for h in range(16)], np.float64)
    t = np.arange(8)
    sbias = np.full((25, 128, 128), NEGB, np.float64)

    def fill(tab, rows, g, valid_extra=None):
        dil = DILS[g]
        dist = 2048 + t[None, :] - rows[:, None]
        ok = (dist >= 0) & (dist % dil == 0) & (dist // dil <= 128)
        if valid_extra is not None:
            ok = ok & valid_extra
        for h in range(16):
            vals = np.where(ok, -sl_h[h] * dist, NEGB)
            sbias[tab, :rows.shape[0], h * 8:(h + 1) * 8] = vals
    fill(0, 1920 + np.arange(128), 0)
    for i in range(4):
        fill(1 + i, 1536 + 128 * i + np.arange(128), 1)
    for r in range(8):
        fill(5 + r, r + 16 * np.arange(128), 2)
    for g in range(3):
        for b in range(4):
            pidx = np.arange(32)
            rows = 2048 + (pidx % 8)
            ok = ((pidx // 8) == b)[:, None] & np.ones((1, 8), bool)
            fill(13 + 4 * g + b, rows, g, ok)
    return ident, psw, nstab, rctab, sbias.astype(np.float32)


_CACHE = {}


def kernel(x_prompt, x_sample, state_pool, cache_k, cache_v, norm_g, ffn_w_gate, ffn_w_up, ffn_w_down,
           pool_w_in, pool_w_grp, pool_scale, pool_w_out, kv_norm, w_k, w_v, attn_w_q, attn_w_o):
    f32 = np.float32
    a = lambda v: np.ascontiguousarray(np.asarray(v, dtype=f32))
    x_prompt, x_sample, state_pool, cache_k, cache_v = map(a, (x_prompt, x_sample, state_pool, cache_k, cache_v))
    if "nc" not in _CACHE:
        _CACHE["nc"] = build_program()
    nc = _CACHE["nc"]
    ident, psw, nstab, rctab, sbias = host_tables()
    vecs = np.concatenate([a(norm_g).reshape(24, D), a(kv_norm).reshape(1, D), a(pool_scale).reshape(2, D)], axis=0)
    gvec = np.ascontiguousarray(vecs.reshape(NVEC, 8, 128).transpose(2, 1, 0)).reshape(128, 8 * NVEC)
    shared = {
        "gvec": gvec, "wgate": a(ffn_w_gate), "wup": a(ffn_w_up), "wdown": a(ffn_w_down), "pwin": a(pool_w_in),
        "pwgrp": a(pool_w_grp), "pwout": a(pool_w_out), "wk": a(w_k), "wv": a(w_v), "wq": a(attn_w_q), "wo": a(attn_w_o),
        "ident": ident, "psw": psw, "nstab": nstab, "rctab": rctab, "sbias": sbias,
    }
    in_maps = []
    for i in range(8):
        xs = x_sample[4 * i:4 * i + 4].reshape(NSM, D)
        xT = np.ascontiguousarray(np.concatenate([x_prompt[i], xs], axis=0).T)
        sp = state_pool[:, 4 * i:4 * i + 4]
        spT = np.ascontiguousarray(sp.transpose(0, 3, 1, 2)).reshape(2, D, 60)
        sprow = np.ascontiguousarray(sp[:, :, 8:15, :])
        m = dict(shared)
        m.update({"xT": xT, "spT": spT, "sprow": sprow,
                  "ck": np.ascontiguousarray(cache_k[4 * i:4 * i + 4].reshape(4, S, D)),
                  "cv": np.ascontiguousarray(cache_v[4 * i:4 * i + 4].reshape(4, S, D))})
        in_maps.append(m)
    res = run_bass_kernel_spmd(nc, in_maps, core_ids=list(range(8)))
    R = res.results
    y_prompt = np.stack([R[i]["yT"][:, :S].T for i in range(8)], axis=0)
    y_sample = np.concatenate([R[i]["yT"][:, S:].T.reshape(4, 8, D) for i in range(8)], axis=0)
    pool_p = np.stack([R[i]["pool_p"] for i in range(8)], axis=1)
    k_p = np.stack([R[i]["k_p"].reshape(S, 16, 64) for i in range(8)], axis=0)
    v_p = np.stack([R[i]["v_p"].reshape(S, 16, 64) for i in range(8)], axis=0)
    pool_s = np.concatenate([R[i]["pool_s"] for i in range(8)], axis=1)
    k_s = np.concatenate([R[i]["k_s"].reshape(4, S, 16, 64) for i in range(8)], axis=0)
    v_s = np.concatenate([R[i]["v_s"].reshape(4, S, 16, 64) for i in range(8)], axis=0)
    c = lambda v: np.ascontiguousarray(v, dtype=f32)
    return (c(y_prompt), c(y_sample), c(pool_p), c(k_p), c(v_p), c(pool_s), c(k_s), c(v_s))
```
